# Optimizing a Trainium2 kernel written in Bass

```python
import jax, jax.numpy as jnp
from jax import lax
import numpy as np

D_MODEL = 1024
BATCH = 8
SEQ = 2048
DEPTH = 2
DEC_BATCH = 128
DEC_SEQ = 8
PAST_LEN = 16384
PAGE_SIZE = 128

N_EVEN = (DEPTH + 1) // 2
N_ODD = DEPTH // 2
D_POOL = D_MODEL // 2
POOL_WINDOWS = (2, 4, 8, 16)
N_POOL_GROUPS = len(POOL_WINDOWS)
POOL_GROUP_DIM = D_POOL // N_POOL_GROUPS
POOL_BUF = max(POOL_WINDOWS) - 1
D_DW = D_MODEL // 2
DW_WIDTH = 31
D_SC = D_MODEL
SC_WIDTH = 3
D_FF = 2816
PLE_DIM = 256
NORM_EPS = 1e-6
LN_EPS = 1e-5

kernel_name = "hybrid_pool_conformer_shortconv_decoder_step"


def rmsnorm(x, g):
    xf = x.astype(jnp.float32)
    inv = lax.rsqrt(jnp.mean(xf * xf, axis=-1, keepdims=True) + NORM_EPS)
    return (xf * inv * g.astype(jnp.float32)).astype(x.dtype)


def layernorm(x, g, b):
    xf = x.astype(jnp.float32)
    mu = jnp.mean(xf, axis=-1, keepdims=True)
    var = jnp.mean(jnp.square(xf - mu), axis=-1, keepdims=True)
    y = (xf - mu) * lax.rsqrt(var + LN_EPS) * g.astype(jnp.float32) + b.astype(jnp.float32)
    return y.astype(x.dtype)


def swiglu(h, w_gate_up, w_down):
    gu = h @ w_gate_up
    gate, up = gu[..., :D_FF], gu[..., D_FF:]
    return (jax.nn.silu(gate) * up) @ w_down


def causal_depthwise_conv(x, prefix, weight):
    k = weight.shape[0]
    c = x.shape[-1]
    xp = jnp.concatenate([prefix.astype(x.dtype), x], axis=1)
    y = lax.conv_general_dilated(
        xp, weight[:, None, :].astype(x.dtype), window_strides=(1,), padding='VALID',
        dimension_numbers=('NWC', 'WIO', 'NWC'), feature_group_count=c)
    return y, xp[:, xp.shape[1] - (k - 1):]


def causal_multiscale_pool(x, prefix, start):
    b, s, c = x.shape
    xcat = jnp.concatenate([prefix.astype(x.dtype), x], axis=1)
    xf = xcat.astype(jnp.float32)
    cs = jnp.concatenate([jnp.zeros((b, 1, c), jnp.float32), jnp.cumsum(xf, axis=1)], axis=1)
    end = cs[:, POOL_BUF + 1:POOL_BUF + 1 + s]
    cur = xf[:, POOL_BUF:]
    pos = start + jnp.arange(s, dtype=jnp.int32)
    outs = []
    for g, w in enumerate(POOL_WINDOWS):
        lo, hi = g * POOL_GROUP_DIM, (g + 1) * POOL_GROUP_DIM
        beg = cs[:, POOL_BUF + 1 - w:POOL_BUF + 1 - w + s, lo:hi]
        cnt = jnp.minimum(w, pos + 1).astype(jnp.float32)[None, :, None]
        outs.append((end[..., lo:hi] - beg) / cnt - cur[..., lo:hi])
    pooled = jnp.concatenate(outs, axis=-1).astype(x.dtype)
    return pooled, xcat[:, xcat.shape[1] - POOL_BUF:]


def even_mixer(h, pool_prefix, dw_prefix, start, w_in, pool_proj, pool_scale,
               dw_weight, dw_bias, dw_ln_gain, dw_ln_bias, w_out):
    b, s, _ = h.shape
    proj = h @ w_in
    xa = proj[..., :D_POOL]
    va = proj[..., D_POOL:D_POOL + D_DW]
    ga = proj[..., D_POOL + D_DW:]
    pooled, new_pool = causal_multiscale_pool(xa, pool_prefix, start)
    pa = jnp.einsum('bsgc,gcd->bsgd', pooled.reshape(b, s, N_POOL_GROUPS, POOL_GROUP_DIM), pool_proj)
    pa = pa.reshape(b, s, D_POOL) * pool_scale
    u = va * jax.nn.sigmoid(ga)
    conv, new_dw = causal_depthwise_conv(u, dw_prefix, dw_weight)
    yb = jax.nn.silu(layernorm(conv + dw_bias, dw_ln_gain, dw_ln_bias))
    out = jnp.concatenate([pa, yb], axis=-1) @ w_out
    return out, new_pool, new_dw


def odd_mixer(h, sc_prefix, w_in, sc_weight, w_out):
    proj = h @ w_in
    gb = proj[..., :D_SC]
    gc = proj[..., D_SC:2 * D_SC]
    xv = proj[..., 2 * D_SC:]
    conv, new_sc = causal_depthwise_conv(gc * xv, sc_prefix, sc_weight)
    return (gb * conv) @ w_out, new_sc


def trunk(x, p, pool_states, dw_states, sc_states, start,
          norm_ffn, w_ffn_gate_up, w_ffn_down, norm_mix,
          w_in_even, pool_proj, pool_scale, dw_weight, dw_bias, dw_ln_gain, dw_ln_bias, w_out_even,
          w_in_odd, sc_weight, w_out_odd,
          norm_ple, w_ple_gate, w_ple_proj, norm_ple_proj, norm_final):
    new_pool, new_dw, new_sc = [], [], []
    for i in range(DEPTH):
        x = x + 0.5 * swiglu(rmsnorm(x, norm_ffn[i, 0]), w_ffn_gate_up[i, 0], w_ffn_down[i, 0])
        h = rmsnorm(x, norm_mix[i])
        if i % 2 == 0:
            j = i // 2
            mix, sp, sd = even_mixer(h, pool_states[j], dw_states[j], start, w_in_even[j], pool_proj[j],
                                     pool_scale[j], dw_weight[j], dw_bias[j], dw_ln_gain[j], dw_ln_bias[j],
                                     w_out_even[j])
            new_pool.append(sp)
            new_dw.append(sd)
        else:
            j = i // 2
            mix, ss = odd_mixer(h, sc_states[j], w_in_odd[j], sc_weight[j], w_out_odd[j])
            new_sc.append(ss)
        x = x + mix
        x = x + 0.5 * swiglu(rmsnorm(x, norm_ffn[i, 1]), w_ffn_gate_up[i, 1], w_ffn_down[i, 1])
        gate = jax.nn.sigmoid(rmsnorm(x, norm_ple[i]) @ w_ple_gate[i])
        emb = rmsnorm(p[i].astype(x.dtype) @ w_ple_proj[i], norm_ple_proj[i])
        x = x + gate * emb
    y = rmsnorm(x, norm_final)
    return y, jnp.stack(new_pool), jnp.stack(new_dw), jnp.stack(new_sc)


def setup_inputs(seed: int = 0) -> dict:
    key = jax.random.key(seed)
    ks = jax.random.split(key, 32)
    f32 = jnp.float32

    def nrm(k, shape, scale=1.0):
        return jax.random.normal(k, shape, f32) * scale

    def gain(k, shape):
        return 1.0 + 0.05 * jax.random.normal(k, shape, f32)

    return {
        "x_prompt": nrm(ks[0], (BATCH, SEQ, D_MODEL)),
        "x_sample": nrm(ks[1], (DEC_BATCH, DEC_SEQ, D_MODEL)),
        "state_pool": nrm(ks[2], (N_EVEN, DEC_BATCH, POOL_BUF, D_POOL)),
        "state_dwconv": nrm(ks[3], (N_EVEN, DEC_BATCH, DW_WIDTH - 1, D_DW)),
        "state_shortconv": nrm(ks[4], (N_ODD, DEC_BATCH, SC_WIDTH - 1, D_SC)),
        "p_prompt": nrm(ks[5], (DEPTH, BATCH, SEQ, PLE_DIM)),
        "p_sample": nrm(ks[6], (DEPTH, DEC_BATCH, DEC_SEQ, PLE_DIM)),
        "norm_ffn": gain(ks[7], (DEPTH, 2, D_MODEL)),
        "w_ffn_gate_up": nrm(ks[8], (DEPTH, 2, D_MODEL, 2 * D_FF), D_MODEL ** -0.5),
        "w_ffn_down": nrm(ks[9], (DEPTH, 2, D_FF, D_MODEL), D_FF ** -0.5),
        "norm_mix": gain(ks[10], (DEPTH, D_MODEL)),
        "w_in_even": nrm(ks[11], (N_EVEN, D_MODEL, D_POOL + 2 * D_DW), D_MODEL ** -0.5),
        "pool_proj": nrm(ks[12], (N_EVEN, N_POOL_GROUPS, POOL_GROUP_DIM, POOL_GROUP_DIM), POOL_GROUP_DIM ** -0.5),
        "pool_scale": gain(ks[13], (N_EVEN, D_POOL)),
        "dw_weight": nrm(ks[14], (N_EVEN, DW_WIDTH, D_DW), DW_WIDTH ** -0.5),
        "dw_bias": nrm(ks[15], (N_EVEN, D_DW), 0.02),
        "dw_ln_gain": gain(ks[16], (N_EVEN, D_DW)),
        "dw_ln_bias": nrm(ks[17], (N_EVEN, D_DW), 0.02),
        "w_out_even": nrm(ks[18], (N_EVEN, D_POOL + D_DW, D_MODEL), (D_POOL + D_DW) ** -0.5),
        "w_in_odd": nrm(ks[19], (N_ODD, D_MODEL, 3 * D_SC), D_MODEL ** -0.5),
        "sc_weight": nrm(ks[20], (N_ODD, SC_WIDTH, D_SC), SC_WIDTH ** -0.5),
        "w_out_odd": nrm(ks[21], (N_ODD, D_SC, D_MODEL), D_SC ** -0.5),
        "norm_ple": gain(ks[22], (DEPTH, D_MODEL)),
        "w_ple_gate": nrm(ks[23], (DEPTH, D_MODEL, D_MODEL), D_MODEL ** -0.5),
        "w_ple_proj": nrm(ks[24], (DEPTH, PLE_DIM, D_MODEL), PLE_DIM ** -0.5),
        "norm_ple_proj": gain(ks[25], (DEPTH, D_MODEL)),
        "norm_final": gain(ks[26], (D_MODEL,)),
    }


def reference(x_prompt, x_sample, state_pool, state_dwconv, state_shortconv, p_prompt, p_sample,
              norm_ffn, w_ffn_gate_up, w_ffn_down, norm_mix,
              w_in_even, pool_proj, pool_scale, dw_weight, dw_bias, dw_ln_gain, dw_ln_bias, w_out_even,
              w_in_odd, sc_weight, w_out_odd,
              norm_ple, w_ple_gate, w_ple_proj, norm_ple_proj, norm_final):
    weights = (norm_ffn, w_ffn_gate_up, w_ffn_down, norm_mix,
               w_in_even, pool_proj, pool_scale, dw_weight, dw_bias, dw_ln_gain, dw_ln_bias, w_out_even,
               w_in_odd, sc_weight, w_out_odd,
               norm_ple, w_ple_gate, w_ple_proj, norm_ple_proj, norm_final)
    bp = x_prompt.shape[0]
    dt = x_prompt.dtype
    zero_pool = jnp.zeros((N_EVEN, bp, POOL_BUF, D_POOL), dt)
    zero_dw = jnp.zeros((N_EVEN, bp, DW_WIDTH - 1, D_DW), dt)
    zero_sc = jnp.zeros((N_ODD, bp, SC_WIDTH - 1, D_SC), dt)
    y_prompt, pool_p, dw_p, sc_p = trunk(x_prompt, p_prompt, zero_pool, zero_dw, zero_sc, 0, *weights)
    y_sample, pool_s, dw_s, sc_s = trunk(x_sample, p_sample, state_pool, state_dwconv, state_shortconv,
                                         PAST_LEN, *weights)
    return (y_prompt, y_sample, pool_p, pool_s, dw_p, dw_s, sc_p, sc_s)
```

```python
import numpy as np
import concourse.bass as bass
import concourse.mybir as mybir
from concourse.bass_utils import run_bass_kernel_spmd

F32 = mybir.dt.float32
BF16 = mybir.dt.bfloat16
AF = mybir.ActivationFunctionType
ALU = mybir.AluOpType

NCORES = 8
D = 1024
DFF = 2816
NHC = DFF // 128
SEQ = 2048
NS = 16
TS = 8
PLE = 256
GCOLS = [1024 + NS * TS, 1024]
GOFF = [0, GCOLS[0]]
TCOLS = sum(GCOLS)
MAXC = max(GCOLS)
SLOT = 3072
NSLOT = 6
NORM_EPS = 1e-6
LN_EPS = 1e-5
HPARTS = [list(range(0, 11)), list(range(11, 22))]

PCOL = {}
_pc = 0


def _padd(name, n):
    global _pc
    PCOL[name] = _pc
    _pc += n


for _l in range(2):
    for _f in range(2):
        _padd(f"norm_ffn{_l}{_f}", 8)
for _l in range(2):
    _padd(f"norm_mix{_l}", 8)
    _padd(f"norm_ple{_l}", 8)
    _padd(f"norm_pp{_l}", 8)
_padd("norm_final", 8)
_padd("pool_scale", 4)
_padd("dw_bias", 4)
_padd("ln_gain", 4)
_padd("ln_bias", 4)
_padd("dw_w", 4 * 31)
_padd("sc_w", 8 * 3)
_padd("ident", 128)
NPAR = _pc


def weight_tile_list():
    tiles = []
    for l in range(2):
        def ffn(f):
            for hp, part in enumerate(HPARTS):
                for m in part:
                    tiles.append((("gu", l, f, m), 8 * 256))
                for i in range(0, len(part), 3):
                    ch = part[i:i + 3]
                    tiles.append((("dn", l, f, tuple(ch)), len(ch) * 1024))
        ffn(0)
        if l == 0:
            for j in range(4):
                tiles.append((("ine_vg", j), 8 * 256))
            for j in range(2):
                tiles.append((("ine_xa", j), 8 * 256))
            tiles.append((("pp",), 4 * 128))
            for ch in ((0, 1, 2), (3, 4, 5), (6, 7)):
                tiles.append((("oute", ch), 8 * 128 * len(ch)))
        else:
            for c in range(8):
                tiles.append((("ino", c), 8 * 384))
            for ch in ((0, 1, 2), (3, 4, 5), (6, 7)):
                tiles.append((("outo", ch), 8 * 128 * len(ch)))
        tiles.append((("plep", l), 2 * 1024))
        ffn(1)
        for ch in ((0, 1, 2), (3, 4, 5), (6, 7)):
            tiles.append((("pleg", l, ch), 8 * 128 * len(ch)))
    return tiles


def _kc(w):
    k, n = w.shape
    return w.reshape(k // 128, 128, n).transpose(1, 0, 2)


def pack_weights(inp):
    tiles = weight_tile_list()
    tot = sum(n for _, n in tiles)
    out = np.empty((128, tot), np.float32)
    off = 0
    for key, n in tiles:
        kind = key[0]
        if kind == "gu":
            _, l, f, m = key
            w = inp["w_ffn_gate_up"][l, f]
            t = np.concatenate([_kc(w[:, m * 128:(m + 1) * 128]),
                                _kc(w[:, DFF + m * 128:DFF + (m + 1) * 128])], axis=2)
        elif kind == "dn":
            _, l, f, ch = key
            w = inp["w_ffn_down"][l, f]
            t = np.stack([w[c * 128:(c + 1) * 128, :] for c in ch], axis=1)
        elif kind == "ine_vg":
            j = key[1]
            w = inp["w_in_even"][0]
            t = np.concatenate([_kc(w[:, 512 + j * 128:512 + (j + 1) * 128]),
                                _kc(w[:, 1024 + j * 128:1024 + (j + 1) * 128])], axis=2)
        elif kind == "ine_xa":
            j = key[1]
            w = inp["w_in_even"][0]
            t = _kc(w[:, j * 256:(j + 1) * 256])
        elif kind == "pp":
            t = inp["pool_proj"][0].transpose(1, 0, 2)
        elif kind == "oute":
            ch = key[1]
            t = _kc(inp["w_out_even"][0][:, ch[0] * 128:(ch[-1] + 1) * 128])
        elif kind == "ino":
            c = key[1]
            w = inp["w_in_odd"][0]
            t = np.concatenate([_kc(w[:, c * 128:(c + 1) * 128]),
                                _kc(w[:, 1024 + c * 128:1024 + (c + 1) * 128]),
                                _kc(w[:, 2048 + c * 128:2048 + (c + 1) * 128])], axis=2)
        elif kind == "outo":
            ch = key[1]
            t = _kc(inp["w_out_odd"][0][:, ch[0] * 128:(ch[-1] + 1) * 128])
        elif kind == "plep":
            t = _kc(inp["w_ple_proj"][key[1]])
        elif kind == "pleg":
            _, l, ch = key
            t = _kc(inp["w_ple_gate"][l][:, ch[0] * 128:(ch[-1] + 1) * 128])
        else:
            raise KeyError(key)
        t = np.ascontiguousarray(t).reshape(128, -1)
        assert t.shape[1] == n, (key, t.shape, n)
        out[:, off:off + n] = t
        off += n
    return out


def pack_params(inp):
    p = np.zeros((128, NPAR), np.float32)

    def put(name, vec):
        v = np.asarray(vec, np.float32).reshape(-1, 128).T
        p[:, PCOL[name]:PCOL[name] + v.shape[1]] = v

    for l in range(2):
        for f in range(2):
            put(f"norm_ffn{l}{f}", inp["norm_ffn"][l, f])
        put(f"norm_mix{l}", inp["norm_mix"][l])
        put(f"norm_ple{l}", inp["norm_ple"][l])
        put(f"norm_pp{l}", inp["norm_ple_proj"][l])
    put("norm_final", inp["norm_final"])
    put("pool_scale", inp["pool_scale"][0])
    put("dw_bias", inp["dw_bias"][0])
    put("ln_gain", inp["dw_ln_gain"][0])
    put("ln_bias", inp["dw_ln_bias"][0])
    dw = inp["dw_weight"][0]
    p[:, PCOL["dw_w"]:PCOL["dw_w"] + 124] = dw.T.reshape(4, 128, 31).transpose(1, 0, 2).reshape(128, 124)
    sc = inp["sc_weight"][0]
    p[:, PCOL["sc_w"]:PCOL["sc_w"] + 24] = sc.T.reshape(8, 128, 3).transpose(1, 0, 2).reshape(128, 24)
    p[:, PCOL["ident"]:PCOL["ident"] + 128] = np.eye(128, dtype=np.float32)
    return p


def _fm(a):
    t, f = a.shape
    return a.T.reshape(f // 128, 128, t).transpose(1, 0, 2)


def pack_core(inp, i):
    xp = inp["x_prompt"][i]
    xs = inp["x_sample"][i * NS:(i + 1) * NS]
    xs_tm = xs.transpose(1, 0, 2).reshape(NS * TS, D)
    xT = np.concatenate([_fm(xp[:1024]), _fm(xs_tm), _fm(xp[1024:])], axis=2)
    pts = []
    for l in range(2):
        pp = inp["p_prompt"][l, i]
        ps = inp["p_sample"][l, i * NS:(i + 1) * NS].transpose(1, 0, 2).reshape(NS * TS, PLE)
        pts.append(np.concatenate([_fm(pp[:1024]), _fm(ps), _fm(pp[1024:])], axis=2))
    pT = np.stack(pts, axis=1)

    def st(a):
        s, r, c = a.shape
        return np.ascontiguousarray(a.transpose(2, 1, 0).reshape(c // 128, 128, r * s).transpose(1, 0, 2))

    return {
        "xT": np.ascontiguousarray(xT, dtype=np.float32),
        "pT": np.ascontiguousarray(pT.reshape(128, 4, TCOLS), dtype=np.float32),
        "st_pool": st(inp["state_pool"][0, i * NS:(i + 1) * NS]),
        "st_dw": st(inp["state_dwconv"][0, i * NS:(i + 1) * NS]),
        "st_sc": st(inp["state_shortconv"][0, i * NS:(i + 1) * NS]),
    }


ENGS = ("pe", "act", "dve", "pool", "sp")
EPOCH = 30000


class Sched:
    def __init__(self):
        self.q = {e: [] for e in ENGS}
        self.cnt = {e: 0 for e in ENGS}
        self.segs = {}
        self.waited = {e: {} for e in ENGS}
        self.dma_cnt = {}
        self.arena = {}
        self.sem_names = set()
        self.label = ""
        self.pe_labels = []

    def reg(self, handle, arena, base):
        self.arena[handle.name] = (arena, base, mybir.dt.size(handle.dtype))

    def region(self, ap):
        name = ap.tensor.name
        if name not in self.arena:
            return None
        arena, base, es = self.arena[name]
        pat = list(ap.ap)
        pstride = pat[0][0]
        off = ap.offset % pstride if pstride > 0 else ap.offset
        ext = 1
        for s, c in pat[1:]:
            ext += (c - 1) * s
        return arena, base + off * es, base + (off + ext) * es

    def _access(self, arena, lo, hi, ticket, is_write, deps):
        segs = self.segs.setdefault(arena, [])
        new = []
        covered = []
        for sg in segs:
            slo, shi, w, rs = sg
            if shi <= lo or slo >= hi:
                new.append(sg)
                continue
            if w is not None:
                deps.add(w)
            if is_write:
                deps.update(rs)
            if slo < lo:
                new.append([slo, lo, w, list(rs)])
            if shi > hi:
                new.append([hi, shi, w, list(rs)])
            olo, ohi = max(slo, lo), min(shi, hi)
            if not is_write:
                new.append([olo, ohi, w, rs + [ticket]])
                covered.append((olo, ohi))
        if is_write:
            new.append([lo, hi, ticket, []])
        else:
            covered.sort()
            cur = lo
            for olo, ohi in covered:
                if olo > cur:
                    new.append([cur, olo, None, [ticket]])
                cur = max(cur, ohi)
            if cur < hi:
                new.append([cur, hi, None, [ticket]])
        self.segs[arena] = new

    def op(self, eng, fn, reads=(), writes=(), dma_sem=None, extra_deps=(), dma_total=None):
        if dma_sem is not None:
            n = self.dma_cnt.get(dma_sem, 0) + 1
            self.dma_cnt[dma_sem] = n
            ticket = (dma_sem, 0, 16 * (dma_total if dma_total is not None else n))
            self.sem_names.add(dma_sem)
        else:
            self.cnt[eng] += 1
            ep, v = divmod(self.cnt[eng] - 1, EPOCH)
            ticket = (f"{eng}_{ep}", ep, v + 1)
            self.sem_names.add(ticket[0])
        deps = set(extra_deps)
        for ap in reads:
            r = self.region(ap)
            if r is not None:
                self._access(r[0], r[1], r[2], ticket, False, deps)
        for ap in writes:
            r = self.region(ap)
            if r is not None:
                self._access(r[0], r[1], r[2], ticket, True, deps)
        deps.discard(ticket)
        best = {}
        for (s, ep, v) in deps:
            key = s.rsplit("_", 1)[0] if s.split("_")[0] in ENGS else s
            if key not in best or (ep, v) > best[key][1:]:
                best[key] = (s, ep, v)
        waits = []
        for key, (s, ep, v) in best.items():
            if eng == "pe" and key == "pe":
                continue
            if dma_sem is None and key == eng and eng == "pe":
                continue
            prev = self.waited[eng].get(key)
            if prev is not None and prev >= (ep, v):
                continue
            self.waited[eng][key] = (ep, v)
            waits.append((s, v))
        self.q[eng].append((waits, fn, ticket))
        return ticket


def build_program():
    nc = bass.Bass("TRN2", target_bir_lowering=False)
    S = Sched()
    wtiles = weight_tile_list()
    WTOT = sum(n for _, n in wtiles)

    d_xT = nc.dram_tensor("xT", [128, 8, TCOLS], F32, kind="ExternalInput").ap()
    d_pT = nc.dram_tensor("pT", [128, 4, TCOLS], F32, kind="ExternalInput").ap()
    d_stp = nc.dram_tensor("st_pool", [128, 4, 15 * NS], F32, kind="ExternalInput").ap()
    d_stdw = nc.dram_tensor("st_dw", [128, 4, 30 * NS], F32, kind="ExternalInput").ap()
    d_stsc = nc.dram_tensor("st_sc", [128, 8, 2 * NS], F32, kind="ExternalInput").ap()
    d_par = nc.dram_tensor("params", [128, NPAR], F32, kind="ExternalInput").ap()
    d_gpr = nc.dram_tensor("gpprow", [128, 2, D], F32, kind="ExternalInput").ap()
    d_w = nc.dram_tensor("wts", [128, WTOT], F32, kind="ExternalInput").ap()
    d_yT = nc.dram_tensor("yT", [128, 8, TCOLS], F32, kind="ExternalOutput").ap()
    d_o_pp = nc.dram_tensor("o_pool_p", [128, 4, 15], F32, kind="ExternalOutput").ap()
    d_o_ps = nc.dram_tensor("o_pool_s", [128, 4, 15 * NS], F32, kind="ExternalOutput").ap()
    d_o_dp = nc.dram_tensor("o_dw_p", [128, 4, 30], F32, kind="ExternalOutput").ap()
    d_o_ds = nc.dram_tensor("o_dw_s", [128, 4, 30 * NS], F32, kind="ExternalOutput").ap()
    d_o_sp = nc.dram_tensor("o_sc_p", [128, 8, 2], F32, kind="ExternalOutput").ap()
    d_o_ss = nc.dram_tensor("o_sc_s", [128, 8, 2 * NS], F32, kind="ExternalOutput").ap()

    cur = [(nc.sbuf_base + 63) // 64 * 64]
    top = nc.sbuf_top

    def alloc(name, shape, dt, arena=None, at=None):
        nbytes = int(np.prod(shape[1:])) * mybir.dt.size(dt)
        if at is None:
            off = cur[0]
            cur[0] += (nbytes + 63) // 64 * 64
            assert cur[0] <= top, (name, cur[0], top)
        else:
            off = at
        h = nc.alloc_sbuf_tensor_at(name, list(shape), dt, offset=off)
        S.reg(h, arena or "sb", off)
        return h

    X = alloc("x", [128, 8, MAXC], F32)
    H = alloc("h", [128, 8, MAXC], BF16)
    HID_BYTES = 11 * MAXC * 2
    hid_off = cur[0]
    HID = alloc("hid", [128, 11, MAXC], BF16)
    UXP = alloc("uxp", [128, 4, 30 + 1024], BF16, at=hid_off)
    o1 = (hid_off + 4 * (30 + 1024) * 2 + 63) // 64 * 64
    UXS = alloc("uxs", [128, 4, 38 * NS], BF16, at=o1)
    o2 = (o1 + 4 * 38 * NS * 2 + 63) // 64 * 64
    V = alloc("v", [128, 4, 512], F32, at=o2)
    assert o2 + 4 * 512 * 4 <= hid_off + HID_BYTES
    YCAT = alloc("ycat", [128, 8, MAXC], BF16, at=hid_off)
    RING = alloc("ring", [128, NSLOT, SLOT], BF16)
    DIAG = alloc("diag", [128, 4 * 31, 128], BF16)
    PT = alloc("pt", [128, 4, MAXC], BF16)
    PAR = alloc("par", [128, NPAR], F32)
    GPPH = alloc("gpph", [128, 16], F32)
    ONES = alloc("ones", [128, 128], BF16)
    IDB = alloc("idb", [128, 128], BF16)
    RC = alloc("rc", [128, 16], F32)
    vb_off = cur[0]
    VB = alloc("vb", [128, 4, 512], BF16)
    VSQ = alloc("vsq", [128, 4, 512], BF16)
    ESQ2 = alloc("esq2", [128, 8, 512], BF16, at=vb_off)
    STP = alloc("stp", [128, 4, 15 * NS], F32)
    STSC = alloc("stsc", [128, 8, 2 * NS], F32)
    PTAIL = alloc("ptail", [128, 4, 16], F32)
    UTAIL = alloc("utail", [128, 4, 32], BF16)
    ZTAIL = alloc("ztail", [128, 8, 2], F32)
    O_PS = alloc("o_ps", [128, 4, TS * NS], F32)
    O_DS = alloc("o_ds", [128, 4, TS * NS], F32)
    O_DP = alloc("o_dp", [128, 4, 32], F32)
    O_SS = alloc("o_ss", [128, 8, 2 * NS], F32)
    NSCR = 10
    SCW = 528
    SCR = [alloc(f"scr{i}", [128, SCW], F32) for i in range(NSCR)]
    RSE = alloc("rse", [128, MAXC], F32)
    LRS = alloc("lrs", [128, 512], F32)
    LRS2 = alloc("lrs2", [128, 512], F32)
    NRS = [alloc("nrs0", [128, 512], F32)] * 2
    nrs_i = [0]
    scr_i = [0]

    def scratch():
        t = SCR[scr_i[0] % NSCR]
        scr_i[0] += 1
        return t

    PS = []
    for i in range(8):
        p = nc.alloc_psum_tensor(f"ps{i}", [128, 512], F32)
        S.arena[p.name] = ("ps", i * 2048, 4)
        PS.append(p)
    ps_i = [0]

    def bank():
        p = PS[ps_i[0] % 8]
        ps_i[0] += 1
        return p

    def pcol(name, c=0, n=1):
        return PAR[:, PCOL[name] + c:PCOL[name] + c + n]

    def act(out, in_, func, bias=None, scale=None, extra_reads=()):
        kw = {}
        rd = [in_] + list(extra_reads)
        if bias is not None:
            kw["bias"] = bias
            if not isinstance(bias, float):
                rd.append(bias)
        if scale is not None:
            kw["scale"] = scale
            if not isinstance(scale, float):
                rd.append(scale)
        S.op("act", lambda e: e.activation(out=out, in_=in_, func=func, **kw), reads=rd, writes=[out])

    def dve_tt(out, in0, in1, op):
        S.op("dve", lambda e: e.tensor_tensor(out=out, in0=in0, in1=in1, op=op), reads=[in0, in1], writes=[out])

    def dve_stt(out, in0, scalar, in1, op0, op1):
        rd = [in0, in1] + ([] if isinstance(scalar, float) else [scalar])
        S.op("dve", lambda e: e.scalar_tensor_tensor(out=out, in0=in0, scalar=scalar, in1=in1, op0=op0, op1=op1),
             reads=rd, writes=[out])

    def dve_ts(out, in0, s1, s2, op0, op1=None):
        rd = [in0] + [s for s in (s1, s2) if s is not None and not isinstance(s, float)]
        if op1 is None:
            S.op("dve", lambda e: e.tensor_scalar(out=out, in0=in0, scalar1=s1, scalar2=None, op0=op0),
                 reads=rd, writes=[out])
        else:
            S.op("dve", lambda e: e.tensor_scalar(out=out, in0=in0, scalar1=s1, scalar2=s2, op0=op0, op1=op1),
                 reads=rd, writes=[out])

    def pool_tt(out, in0, in1, op):
        S.op("pool", lambda e: e.tensor_tensor(out=out, in0=in0, in1=in1, op=op), reads=[in0, in1], writes=[out])

    def dve_copy(out, in_):
        S.op("dve", lambda e: e.tensor_copy(out=out, in_=in_), reads=[in_], writes=[out])

    def dve_recip(out, in_):
        S.op("dve", lambda e: e.reciprocal(out=out, in_=in_), reads=[in_], writes=[out])

    def dve_memset(out, val):
        S.op("dve", lambda e: e.memset(out, val), writes=[out])

    PE_FW = [None]

    def mm_group(ps_ap, pairs, extra_reads=()):
        n = len(pairs)

        def fn(e):
            ins = None
            for i, (l, r) in enumerate(pairs):
                ins = e.matmul(ps_ap, l, r, start=(i == 0), stop=(i == n - 1))
                if i == 0 and PE_FW[0] is not None:
                    ins._wait_ge(*PE_FW[0])
                    PE_FW[0] = None
            return ins
        rd = [a for pr in pairs for a in pr] + list(extra_reads)
        S.pe_labels.extend([S.label] * n)
        S.op("pe", fn, reads=rd, writes=[ps_ap])

    def dma(eng, out, in_, sem, total=None, extra=()):
        S.op(eng, lambda e: e.dma_start(out=out, in_=in_), reads=[in_], writes=[out], dma_sem=sem, dma_total=total,
             extra_deps=extra)

    def dma_group(eng, pairs, sem):
        tot = S.dma_cnt.get(sem, 0) + len(pairs)
        for (out, in_) in pairs:
            dma(eng, out, in_, sem, total=tot)

    wt_off = {}
    o = 0
    for key, n in wtiles:
        wt_off[key] = (o, n)
        o += n
    class WStream:
        def __init__(self):
            self.order = [k for k, _ in wtiles]
            self.reset()

        def reset(self):
            self.pos = 0
            self.loaded = {}
            self.free = list(range(NSLOT))

        def _load_next(self):
            k = self.order[self.pos]
            s = self.free.pop(0)
            off, n = wt_off[k]
            extra = (("xsem0", 0, 128),) if not first_w_issued else ()
            first_w_issued.append(1)
            dma("pool", RING[:, s, 0:n], d_w[:, off:off + n], f"wsem{s}", extra=extra)
            self.loaded[k] = s
            self.pos += 1

        def get(self, key, ahead):
            idx = self.order.index(key)
            while self.pos <= idx:
                assert self.free, ("no free weight slot for", key)
                self._load_next()
            while self.pos < len(self.order) and self.pos <= idx + ahead and self.free:
                self._load_next()
            return self.loaded[key]

        def done(self, key):
            self.free.append(self.loaded.pop(key))

    first_w_issued = []
    W = WStream()

    G0_BLOCKS = [(0, 512), (512, 512), (1024, 128)]
    for bi, (c0_, n_) in enumerate(G0_BLOCKS):
        dma_group("sp", [(X[:, c, c0_:c0_ + n_], d_xT[:, c, GOFF[0] + c0_:GOFF[0] + c0_ + n_]) for c in range(8)],
                  f"xsem{bi}")
        if bi == 0:
            dma("sp", PAR[:, :], d_par[:, :], "psem")
    dve_memset(ONES[:, :], 1.0)
    dve_copy(IDB[:, :], PAR[:, PCOL["ident"]:PCOL["ident"] + 128])
    for l in range(2):
        dve_ts(GPPH[:, l * 8:(l + 1) * 8], PAR[:, PCOL[f"norm_pp{l}"]:PCOL[f"norm_pp{l}"] + 8], 0.5, None, ALU.mult)
    late = []
    for t in range(15):
        late.append(lambda t=t: dve_memset(RC[:, t:t + 1], 1.0 / (t + 1)))
    for j in range(4):
        for k in range(31):
            late.append(lambda j=j, k=k: dve_ts(DIAG[:, j * 31 + k, :], IDB[:, :], pcol("dw_w", j * 31 + k), None, ALU.mult))
    late.append(lambda: dma_group("sp", [(STP[:, :, :], d_stp[:, :, :]), (STSC[:, :, :], d_stsc[:, :, :])], "stsem"))
    late.append(lambda: dma("sp", d_o_ps[:, :, 0:7 * NS], d_stp[:, :, 8 * NS:15 * NS], "osem_a"))
    late.append(lambda: dma("sp", d_o_ds[:, :, 0:22 * NS], d_stdw[:, :, 8 * NS:30 * NS], "osem_a"))

    def late_tick(k=4):
        for _ in range(k):
            if late:
                late.pop(0)()

    pending = []
    cur_goff = [0]
    cur_g = [0]

    G1_BLOCKS = [(0, 512), (512, 384), (896, 128)]

    def flush():
        lab = S.label
        while pending:
            pending.pop(0)()
        S.label = lab

    def tick(k=2):
        lab = S.label
        for _ in range(k):
            if not pending:
                break
            pending.pop(0)()
        S.label = lab

    def norm_squares(blk):
        c0, n = blk
        for c in range(8):
            act(H[:, c, c0:c0 + n], X[:, c, c0:c0 + n], AF.Square)

    def norm_steps(blk, gname, final, g_at):
        c0, n = blk
        rs_box = []
        steps = []

        def s_stats():
            S.label = "norm_" + gname
            pb = bank()
            mm_group(pb[:, 0:n], [(ONES[:, :], H[:, c, c0:c0 + n]) for c in range(8)])
            sd = scratch()
            act(sd[:, 0:n], pb[:, 0:n], AF.Sqrt, bias=NORM_EPS, scale=1.0 / D)
            rs = LRS2 if final else NRS[nrs_i[0] % 2]
            nrs_i[0] += 1
            dve_recip(rs[:, 0:n], sd[:, 0:n])
            rs_box.append(rs)
        steps.append(s_stats)

        def mk(c):
            def s_scale():
                rs = rs_box[0]
                if not final:
                    dve_stt(H[:, c, c0:c0 + n], X[:, c, c0:c0 + n], pcol(gname, c), rs[:, 0:n], ALU.mult, ALU.mult)
                else:
                    dve_stt(X[:, c, c0:c0 + n], X[:, c, c0:c0 + n], pcol(gname, c), rs[:, 0:n], ALU.mult, ALU.mult)
                    sem = f"yo{c}_{c0}" if (g_at == 0 and c0 + n <= GCOLS[1]) else "ysem"
                    dma("sp", d_yT[:, c, GOFF[g_at] + c0:GOFF[g_at] + c0 + n], X[:, c, c0:c0 + n], sem)
            return s_scale
        for c in range(8):
            steps.append(mk(c))

        if final and g_at == 0 and c0 + n <= GCOLS[1]:
            def s_next():
                dma_group("sp", [(X[:, c, c0:c0 + n], d_xT[:, c, GOFF[1] + c0:GOFF[1] + c0 + n]) for c in range(8)],
                          f"xg1_{c0}")
                pending.extend([(lambda: None)] * 10)
                for (b0, bn) in G1_BLOCKS:
                    if c0 <= b0 and b0 + bn <= c0 + n:
                        pending.append(lambda b0=b0, bn=bn: x_block_final((b0, bn), "norm_ffn00", g_next=1))
            steps.append(s_next)
        return steps

    def x_block_final(blk, gname, g_next=None):
        norm_squares(blk)
        final = (gname == "norm_final")
        g_at = cur_g[0] if g_next is None else g_next
        pending.extend(norm_steps(blk, gname, final, g_at))

    def tb_order(nt, nb):
        lead = min(4, nt)
        out = [(0, 0), "F"]
        out += [(0, b) for b in range(1, nb - 1)]
        for t in range(1, lead):
            out += [(t, b) for b in range(0, nb - 1)]
        out += [(t, nb - 1) for t in range(lead)]
        for t in range(lead, nt):
            out += [(t, b) for b in range(nb)]
        return out

    def last_use(order):
        lu = {}
        for i, it in enumerate(order):
            if it != "F":
                lu[it[0]] = i
        return lu

    def emb_stats_mm(l, wp, blk):
        c0, n = blk
        for mo in range(8):
            pb = bank()
            mm_group(pb[:, 0:n], [(wp[:, k, mo * 128:(mo + 1) * 128], PT[:, l * 2 + k, c0:c0 + n]) for k in range(2)])
            act(ESQ2[:, mo, 0:n], pb[:, 0:n], AF.Square)

    fin_l = [0]

    def emb_stats_fin(blk):
        c0, n = blk
        pb = bank()
        mm_group(pb[:, 0:n], [(ONES[:, :], ESQ2[:, mo, 0:n]) for mo in range(8)])
        sd = scratch()
        act(sd[:, 0:n], pb[:, 0:n], AF.Sqrt, bias=NORM_EPS, scale=1.0 / D)
        dve_recip(RSE[:, c0:c0 + n], sd[:, 0:n])
        for k in range(2):
            dve_tt(PT[:, fin_l[0] * 2 + k, c0:c0 + n], PT[:, fin_l[0] * 2 + k, c0:c0 + n], RSE[:, c0:c0 + n], ALU.mult)

    def ffn(l, f, blocks, next_g):
        nb = len(blocks)
        wp = None
        if f == 1:
            sp_ = W.get(("plep", l), 2)
            wp = RING[:, sp_, 0:2048].rearrange("p (k n) -> p k n", k=2)
        for pi, part in enumerate(HPARTS):
            S.label = f"gu{l}{f}"
            order = tb_order(len(part), nb) if pi == 0 else [(t, b) for t in range(len(part)) for b in range(nb)]
            lu = last_use(order)
            wts_ = {}
            for oi, item in enumerate(order):
                if item == "F":
                    flush()
                    continue
                j, bi = item
                if j not in wts_:
                    s = W.get(("gu", l, f, part[j]), 2)
                    wts_[j] = RING[:, s, 0:2048].rearrange("p (k n) -> p k n", k=8)
                wt = wts_[j]
                c0, n = blocks[bi]
                pg, pu = bank(), bank()
                mm_group(pg[:, 0:n], [(wt[:, k, 0:128], H[:, k, c0:c0 + n]) for k in range(8)])
                mm_group(pu[:, 0:n], [(wt[:, k, 128:256], H[:, k, c0:c0 + n]) for k in range(8)])
                sg = scratch()
                act(sg[:, 0:n], pg[:, 0:n], AF.Silu)
                dve_tt(HID[:, j, c0:c0 + n], pu[:, 0:n], sg[:, 0:n], ALU.mult)
                if oi >= 3 or pi > 0 or f > 0 or l > 0:
                    late_tick()
                if lu[j] == oi:
                    W.done(("gu", l, f, part[j]))
            S.label = f"dn{l}{f}"
            wdn = {}
            dn_keys = []
            for i in range(0, len(part), 3):
                ch = tuple(part[i:i + 3])
                s = W.get(("dn", l, f, ch), 1)
                dn_keys.append(("dn", l, f, ch))
                for ii, cc in enumerate(ch):
                    wdn[cc] = RING[:, s, ii * 1024:(ii + 1) * 1024]

            def dn_item(mo, blk):
                c0, n = blk
                pb = bank()
                mm_group(pb[:, 0:n], [(wdn[m][:, mo * 128:(mo + 1) * 128], HID[:, j, c0:c0 + n])
                                      for j, m in enumerate(part)])
                dve_stt(X[:, mo, c0:c0 + n], pb[:, 0:n], 0.5, X[:, mo, c0:c0 + n], ALU.mult, ALU.add)

            if pi < len(HPARTS) - 1:
                for mo in range(8):
                    for blk in blocks:
                        dn_item(mo, blk)
                    if f == 1:
                        S.label = f"embst{l}"
                        if 1 <= mo and mo - 1 < nb:
                            fin_l[0] = l
                            emb_stats_fin(blocks[mo - 1])
                        if mo < nb:
                            emb_stats_mm(l, wp, blocks[mo])
                        S.label = f"dn{l}{f}"
            else:
                for blk in blocks:
                    for mo in range(8):
                        dn_item(mo, blk)
                        tick()
                    x_block_final(blk, next_g)
            for k in dn_keys:
                W.done(k)

    def even_mixer(g, blocks, next_g):
        nb = len(blocks)
        late_tick(10 ** 6)
        last_prompt = (g == 1)
        if g == 0:
            dve_memset(UXP[:, :, 0:30], 0.0)
            dma("pool", UXS[:, :, 0:30 * NS], d_stdw[:, :, :], "stsem2")
            dve_memset(PTAIL[:, :, :], 0.0)
        else:
            dve_copy(UXP[:, :, 0:30], UTAIL[:, :, 0:30])
        S.label = "even_vg"
        wts_ = {}
        order = tb_order(4, nb)
        lu = last_use(order)
        for oi, item in enumerate(order):
            if item == "F":
                flush()
                continue
            j, bi = item
            if j not in wts_:
                s = W.get(("ine_vg", j), 2)
                wts_[j] = RING[:, s, 0:2048].rearrange("p (k n) -> p k n", k=8)
            wt = wts_[j]
            c0, n = blocks[bi]
            is_s = (g == 0 and bi == 2)
            pv, pg = bank(), bank()
            mm_group(pv[:, 0:n], [(wt[:, k, 0:128], H[:, k, c0:c0 + n]) for k in range(8)])
            mm_group(pg[:, 0:n], [(wt[:, k, 128:256], H[:, k, c0:c0 + n]) for k in range(8)])
            th, vh = scratch(), scratch()
            act(th[:, 0:n], pg[:, 0:n], AF.Tanh, scale=0.5)
            act(vh[:, 0:n], pv[:, 0:n], AF.Copy, scale=0.5)
            if is_s:
                dst = UXS[:, j, 30 * NS:38 * NS]
            else:
                dst = UXP[:, j, 30 + c0:30 + c0 + n]
            dve_stt(dst, th[:, 0:n], 1.0, vh[:, 0:n], ALU.add, ALU.mult)
            if is_s:
                dve_stt(O_DS[:, j, :], th[:, 0:n], 1.0, vh[:, 0:n], ALU.add, ALU.mult)
            if last_prompt and bi == nb - 1:
                dve_stt(O_DP[:, j, 0:30], th[:, n - 30:n], 1.0, vh[:, n - 30:n], ALU.add, ALU.mult)
            if g == 0 and bi == 1:
                dve_stt(UTAIL[:, j, 0:30], th[:, n - 30:n], 1.0, vh[:, n - 30:n], ALU.add, ALU.mult)
            if lu[j] == oi:
                W.done(("ine_vg", j))
        S.label = "even_xa"
        sp = W.get(("pp",), 0)
        wpp = RING[:, sp, 0:512].rearrange("p (g d) -> p g d", g=4)
        wxa = []
        for jj in range(2):
            s = W.get(("ine_xa", jj), 0)
            wxa.append(RING[:, s, 0:2048].rearrange("p (k n) -> p k n", k=8))

        def pool_item(c, bi, c0, n, pb):
            is_s = (g == 0 and bi == 2)
            st = NS if is_s else 1
            P = 15 * st
            w = 2 << c
            xe = scratch()
            if is_s:
                dve_copy(xe[:, 0:P], STP[:, c, :])
            else:
                dve_copy(xe[:, 0:P], PTAIL[:, c, 0:15])
            act(xe[:, P:P + n], pb[:, 0:n], AF.Copy)
            if is_s:
                dve_copy(O_PS[:, c, :], xe[:, P:P + n])
            else:
                dve_copy(PTAIL[:, c, 0:15], xe[:, n:n + 15])
            L = P + n
            src = xe
            step = 1
            while step < w:
                dst = scratch()
                lo = (2 * step - 1) * st
                dve_tt(dst[:, lo:L], src[:, lo:L], src[:, lo - step * st:L - step * st], ALU.add)
                src = dst
                step *= 2
            dve_stt(H[:, c, c0:c0 + n], src[:, P:L], 1.0 / w, xe[:, P:L], ALU.mult, ALU.subtract)
            if g == 0 and bi == 0:
                tmp = scratch()
                dve_tt(tmp[:, 0:w - 1], src[:, P:P + w - 1], RC[:, 0:w - 1], ALU.mult)
                dve_tt(H[:, c, c0:c0 + w - 1], tmp[:, 0:w - 1], xe[:, P:P + w - 1], ALU.subtract)

        for bi, (c0, n) in enumerate(blocks):
            pbs = []
            for c in range(4):
                pb = bank()
                mm_group(pb[:, 0:n], [(wxa[c // 2][:, k, (c % 2) * 128:(c % 2 + 1) * 128], H[:, k, c0:c0 + n])
                                      for k in range(8)])
                pbs.append(pb)
            for c in range(4):
                pool_item(c, bi, c0, n, pbs[c])
        W.done(("ine_xa", 0))
        W.done(("ine_xa", 1))
        def pool_proj_blk(blk_):
            S.label = "even_pp"
            for (c0, n) in [blk_]:
                for c in range(4):
                    pq = bank()
                    mm_group(pq[:, 0:n], [(wpp[:, c, :], H[:, c, c0:c0 + n])])
                    act(H[:, c, c0:c0 + n], pq[:, 0:n], AF.Copy, scale=pcol("pool_scale", c))
            S.label = "even_conv"

        for ci_, ch_ in enumerate(((0, 1, 2), (3, 4, 5), (6, 7))):
            W.get(("oute", ch_), 0)
        S.label = "even_conv"
        for bi, (c0, n) in enumerate(blocks):
            is_s = (g == 0 and bi == 2)
            for j in range(4):
                pb = bank()
                if is_s:
                    pairs = [(DIAG[:, j * 31 + k, :], UXS[:, j, k * NS:k * NS + n]) for k in range(31)]
                else:
                    pairs = [(DIAG[:, j * 31 + k, :], UXP[:, j, c0 + k:c0 + k + n]) for k in range(31)]
                mm_group(pb[:, 0:n], pairs)
                act(V[:, j, 0:n], pb[:, 0:n], AF.Identity, bias=pcol("dw_bias", j))
                act(VB[:, j, 0:n], pb[:, 0:n], AF.Identity, bias=pcol("dw_bias", j))
                act(VSQ[:, j, 0:n], pb[:, 0:n], AF.Square, bias=pcol("dw_bias", j))
            if bi >= 1:
                pool_proj_blk(blocks[bi - 1])
            if bi == nb - 1:
                pool_proj_blk(blocks[bi])
            p1, p2 = bank(), bank()
            mm_group(p1[:, 0:n], [(ONES[:, :], VB[:, j, 0:n]) for j in range(4)])
            mm_group(p2[:, 0:n], [(ONES[:, :], VSQ[:, j, 0:n]) for j in range(4)])
            mu, msq, var, sd, rs = scratch(), scratch(), scratch(), scratch(), scratch()
            dve_ts(mu[:, 0:n], p1[:, 0:n], 1.0 / 512, None, ALU.mult)
            dve_tt(msq[:, 0:n], mu[:, 0:n], mu[:, 0:n], ALU.mult)
            dve_stt(var[:, 0:n], p2[:, 0:n], 1.0 / 512, msq[:, 0:n], ALU.mult, ALU.subtract)
            act(sd[:, 0:n], var[:, 0:n], AF.Sqrt, bias=LN_EPS)
            dve_recip(rs[:, 0:n], sd[:, 0:n])
            for j in range(4):
                dve_tt(V[:, j, 0:n], V[:, j, 0:n], mu[:, 0:n], ALU.subtract)
                dve_tt(V[:, j, 0:n], V[:, j, 0:n], rs[:, 0:n], ALU.mult)
                act(H[:, 4 + j, c0:c0 + n], V[:, j, 0:n], AF.Silu, bias=pcol("ln_bias", j), scale=pcol("ln_gain", j))
        W.done(("pp",))
        S.label = "even_out"
        wo = {}
        for ci, ch in enumerate(((0, 1, 2), (3, 4, 5), (6, 7))):
            s = W.get(("oute", ch), (2, 1, 1)[ci])
            wt = RING[:, s, 0:8 * 128 * len(ch)].rearrange("p (k n) -> p k n", k=8)
            for ii, mo in enumerate(ch):
                wo[mo] = (wt, ii)
        for blk in blocks:
            c0, n = blk
            for mo in range(8):
                wt, ii = wo[mo]
                pb = bank()
                mm_group(pb[:, 0:n], [(wt[:, k, ii * 128:(ii + 1) * 128], H[:, k, c0:c0 + n]) for k in range(8)])
                dve_tt(X[:, mo, c0:c0 + n], pb[:, 0:n], X[:, mo, c0:c0 + n], ALU.add)
                tick()
            x_block_final(blk, next_g)
        for ch in ((0, 1, 2), (3, 4, 5), (6, 7)):
            W.done(("oute", ch))
        if last_prompt:
            dma("sp", d_o_dp[:, :, :], O_DP[:, :, 0:30], "osem_b")
            dma("sp", d_o_pp[:, :, :], PTAIL[:, :, 0:15], "osem_b")
        if g == 0:
            dma("sp", d_o_ds[:, :, 22 * NS:30 * NS], O_DS[:, :, :], "osem_b")
            dma("sp", d_o_ps[:, :, 7 * NS:15 * NS], O_PS[:, :, :], "osem_b")

    def odd_mixer(g, blocks, next_g):
        nb = len(blocks)
        last_prompt = (g == 1)
        if g == 0:
            dve_memset(ZTAIL[:, :, :], 0.0)
        S.label = "odd_in"
        wts_ = {}
        order = tb_order(8, nb)
        lu = last_use(order)
        for oi, item in enumerate(order):
            if item == "F":
                flush()
                continue
            c, bi = item
            if c not in wts_:
                s = W.get(("ino", c), 2)
                wts_[c] = RING[:, s, 0:3072].rearrange("p (k n) -> p k n", k=8)
            wt = wts_[c]
            c0, n = blocks[bi]
            is_s = (g == 0 and bi == 2)
            st = NS if is_s else 1
            P = 2 * st
            pgb, pgc, pxv = bank(), bank(), bank()
            mm_group(pgb[:, 0:n], [(wt[:, k, 0:128], H[:, k, c0:c0 + n]) for k in range(8)])
            mm_group(pgc[:, 0:n], [(wt[:, k, 128:256], H[:, k, c0:c0 + n]) for k in range(8)])
            mm_group(pxv[:, 0:n], [(wt[:, k, 256:384], H[:, k, c0:c0 + n]) for k in range(8)])
            gcs, gbs, ze, a1, a2 = scratch(), scratch(), scratch(), scratch(), scratch()
            act(gcs[:, 0:n], pgc[:, 0:n], AF.Copy)
            act(gbs[:, 0:n], pgb[:, 0:n], AF.Copy)
            if is_s:
                dve_copy(ze[:, 0:P], STSC[:, c, :])
            else:
                dve_copy(ze[:, 0:P], ZTAIL[:, c, :])
            dve_tt(ze[:, P:P + n], pxv[:, 0:n], gcs[:, 0:n], ALU.mult)
            if is_s:
                dve_copy(O_SS[:, c, :], ze[:, P + 6 * NS:P + 8 * NS])
            else:
                dve_copy(ZTAIL[:, c, :], ze[:, n:n + 2])
            dve_ts(a1[:, 0:n], ze[:, 0:n], pcol("sc_w", c * 3 + 0), None, ALU.mult)
            dve_stt(a2[:, 0:n], ze[:, st:st + n], pcol("sc_w", c * 3 + 1), a1[:, 0:n], ALU.mult, ALU.add)
            dve_stt(a1[:, 0:n], ze[:, 2 * st:2 * st + n], pcol("sc_w", c * 3 + 2), a2[:, 0:n], ALU.mult, ALU.add)
            dve_tt(YCAT[:, c, c0:c0 + n], a1[:, 0:n], gbs[:, 0:n], ALU.mult)
            if lu[c] == oi:
                W.done(("ino", c))
        S.label = "odd_out"
        wo = {}
        for ci, ch in enumerate(((0, 1, 2), (3, 4, 5), (6, 7))):
            s = W.get(("outo", ch), (2, 1, 1)[ci])
            wt = RING[:, s, 0:8 * 128 * len(ch)].rearrange("p (k n) -> p k n", k=8)
            for ii, mo in enumerate(ch):
                wo[mo] = (wt, ii)
        for blk in blocks:
            c0, n = blk
            for mo in range(8):
                wt, ii = wo[mo]
                pb = bank()
                mm_group(pb[:, 0:n], [(wt[:, k, ii * 128:(ii + 1) * 128], YCAT[:, k, c0:c0 + n]) for k in range(8)])
                dve_tt(X[:, mo, c0:c0 + n], pb[:, 0:n], X[:, mo, c0:c0 + n], ALU.add)
                tick()
            x_block_final(blk, next_g)
        for ch in ((0, 1, 2), (3, 4, 5), (6, 7)):
            W.done(("outo", ch))
        if last_prompt:
            dma("sp", d_o_sp[:, :, :], ZTAIL[:, :, :], "osem_b")
        if g == 0:
            dma("sp", d_o_ss[:, :, :], O_SS[:, :, :], "osem_b")

    def ple(l, blocks, next_g):
        S.label = f"ple{l}"
        sp_ = W.get(("plep", l), 0)
        wp = RING[:, sp_, 0:2048].rearrange("p (k n) -> p k n", k=2)
        wg = {}
        gkeys = []
        for ci, ch in enumerate(((0, 1, 2), (3, 4, 5), (6, 7))):
            s = W.get(("pleg", l, ch), (2, 1, 1)[ci])
            gkeys.append(("pleg", l, ch))
            wt = RING[:, s, 0:8 * 128 * len(ch)].rearrange("p (k n) -> p k n", k=8)
            for ii, mo in enumerate(ch):
                wg[mo] = (wt, ii)
        for hh in range(2):
            gr = scratch()
            dma("sp", gr[:, 0:512], d_gpr[:, l, hh * 512:(hh + 1) * 512], f"gpr{hh}")
            dve_ts(gr[:, 0:512], gr[:, 0:512], 0.5, None, ALU.mult)
            for k in range(2):
                dve_tt(wp[:, k, hh * 512:(hh + 1) * 512], wp[:, k, hh * 512:(hh + 1) * 512], gr[:, 0:512], ALU.mult)
        for bi, blk in enumerate(blocks):
            c0, n = blk
            for mo in range(8):
                wt, ii = wg[mo]
                pe_, pz = bank(), bank()
                mm_group(pe_[:, 0:n], [(wp[:, k, mo * 128:(mo + 1) * 128], PT[:, l * 2 + k, c0:c0 + n]) for k in range(2)])
                mm_group(pz[:, 0:n], [(wt[:, k, ii * 128:(ii + 1) * 128], H[:, k, c0:c0 + n]) for k in range(8)])
                th, t1 = scratch(), scratch()
                act(th[:, 0:n], pz[:, 0:n], AF.Tanh, scale=0.5)
                dve_stt(t1[:, 0:n], th[:, 0:n], 1.0, pe_[:, 0:n], ALU.add, ALU.mult)
                pool_tt(X[:, mo, c0:c0 + n], X[:, mo, c0:c0 + n], t1[:, 0:n], ALU.add)
                if bi == 0 and mo == 0:
                    flush()
                else:
                    tick()
            x_block_final(blk, next_g)
        W.done(("plep", l))
        for k in gkeys:
            W.done(k)

    for g in range(2):
        nco = GCOLS[g]
        cur_goff[0] = GOFF[g]
        cur_g[0] = g
        blocks = [(0, 512), (512, 512), (1024, 128)] if g == 0 else G1_BLOCKS
        W.reset()
        if g == 0:
            late.insert(0, lambda: dma_group("pool", [(PT[:, c, 0:GCOLS[0]], d_pT[:, c, GOFF[0]:GOFF[0] + GCOLS[0]])
                                                      for c in range(4)], "ptsem"))
        else:
            dma_group("pool", [(PT[:, c, 0:nco], d_pT[:, c, GOFF[g]:GOFF[g] + nco]) for c in range(4)], "ptsem")
        if g == 0:
            for blk in blocks:
                x_block_final(blk, "norm_ffn00")
            keep = pending[-9:]
            del pending[-9:]
            flush()
            pending.extend(keep)
        for l in range(2):
            ffn(l, 0, blocks, f"norm_mix{l}")
            if l == 0:
                even_mixer(g, blocks, "norm_ffn01")
            else:
                odd_mixer(g, blocks, "norm_ffn11")
            ffn(l, 1, blocks, f"norm_ple{l}")
            ple(l, blocks, "norm_ffn10" if l == 0 else "norm_final")
        flush()
        flush()

    out_sems = ["osem_a", "osem_b"] + sorted(n_ for n_ in S.sem_names if n_.startswith("ysem") or n_.startswith("yo"))
    sem_objs = {}
    sem_list = sorted(S.sem_names)
    import contextlib
    with contextlib.ExitStack() as stack:
        for sname in sem_list:
            sem_objs[sname] = stack.enter_context(nc.semaphore(sname))
        block = stack.enter_context(nc.Block())

        def is_eng_sem(sn):
            return sn.split("_")[0] in ENGS
        ref = {}
        for name_ in ENGS:
            for waits_, _, _ in S.q[name_]:
                for (sn, v) in waits_:
                    if is_eng_sem(sn):
                        ref.setdefault(sn, set()).add(v)
        rank = {sn: {v: i + 1 for i, v in enumerate(sorted(vs))} for sn, vs in ref.items()}

        def mapv(sn, v):
            return rank[sn][v] if is_eng_sem(sn) else v

        def run_engine(e, name):
            for waits, fn, ticket in S.q[name]:
                waits = [(sn, mapv(sn, v)) for (sn, v) in waits]
                fused = None
                if name == "pe" and waits:
                    PE_FW[0] = (sem_objs[waits[-1][0]], waits[-1][1])
                    waits = waits[:-1]
                if (name in ("act", "dve") or (name == "pool" and is_eng_sem(ticket[0]))) and waits:
                    fused = waits[-1]
                    waits = waits[:-1]
                for (sn, v) in waits:
                    e.wait_ge(sem_objs[sn], v)
                ins = fn(e)
                if fused is not None:
                    ins._wait_ge(sem_objs[fused[0]], fused[1])
                if is_eng_sem(ticket[0]):
                    if ticket[2] in ref.get(ticket[0], ()):
                        ins.then_inc(sem_objs[ticket[0]], 1)
                else:
                    ins.then_inc(sem_objs[ticket[0]], 16)
            if name == "sp":
                for sn in out_sems:
                    e.wait_ge(sem_objs[sn], 16 * S.dma_cnt[sn])

        block.tensor(lambda e: run_engine(e, "pe"))
        block.scalar(lambda e: run_engine(e, "act"))
        block.vector(lambda e: run_engine(e, "dve"))
        block.gpsimd(lambda e: run_engine(e, "pool"))
        block.sync(lambda e: run_engine(e, "sp"))
    nc._pe_labels = S.pe_labels
    return nc


_CACHE = {}


def kernel(**inputs):
    inp = {k: np.asarray(v) for k, v in inputs.items()}
    if "nc" not in _CACHE:
        _CACHE["nc"] = build_program()
    nc = _CACHE["nc"]
    wts = pack_weights(inp)
    par = pack_params(inp)
    gpr = np.ascontiguousarray(np.broadcast_to(inp["norm_ple_proj"].astype(np.float32)[None, :, :], (128, 2, D)))
    in_maps = []
    for i in range(NCORES):
        m = pack_core(inp, i)
        m["params"] = par
        m["gpprow"] = gpr
        m["wts"] = wts
        in_maps.append(m)
    res = run_bass_kernel_spmd(nc, in_maps, core_ids=list(range(NCORES)))
    R = res.results

    def unfm(a):
        return a.transpose(2, 1, 0).reshape(a.shape[2], -1)

    y_prompt = np.empty((NCORES, SEQ, D), np.float32)
    y_sample = np.empty((NCORES * NS, TS, D), np.float32)
    pool_p = np.empty((1, NCORES, 15, 512), np.float32)
    pool_s = np.empty((1, NCORES * NS, 15, 512), np.float32)
    dw_p = np.empty((1, NCORES, 30, 512), np.float32)
    dw_s = np.empty((1, NCORES * NS, 30, 512), np.float32)
    sc_p = np.empty((1, NCORES, 2, 1024), np.float32)
    sc_s = np.empty((1, NCORES * NS, 2, 1024), np.float32)
    for i in range(NCORES):
        yT = np.asarray(R[i]["yT"])
        y_prompt[i, :1024] = unfm(yT[:, :, 0:1024])
        y_prompt[i, 1024:] = unfm(yT[:, :, GOFF[1]:GOFF[1] + 1024])
        ys = unfm(yT[:, :, 1024:1024 + NS * TS]).reshape(TS, NS, D).transpose(1, 0, 2)
        y_sample[i * NS:(i + 1) * NS] = ys

        def st_s(a, rows):
            return unfm(np.asarray(a)).reshape(rows, NS, -1).transpose(1, 0, 2)

        pool_p[0, i] = unfm(np.asarray(R[i]["o_pool_p"]))
        dw_p[0, i] = unfm(np.asarray(R[i]["o_dw_p"]))
        sc_p[0, i] = unfm(np.asarray(R[i]["o_sc_p"]))
        pool_s[0, i * NS:(i + 1) * NS] = st_s(R[i]["o_pool_s"], 15)
        dw_s[0, i * NS:(i + 1) * NS] = st_s(R[i]["o_dw_s"], 30)
        sc_s[0, i * NS:(i + 1) * NS] = st_s(R[i]["o_sc_s"], 2)
    return (y_prompt, y_sample, pool_p, pool_s, dw_p, dw_s, sc_p, sc_s)
```

```python
import numpy as np
import concourse.bass as bass
import concourse.mybir as mybir
from concourse.bass_utils import run_bass_kernel_spmd

F32 = mybir.dt.float32
BF16 = mybir.dt.bfloat16
AF = mybir.ActivationFunctionType
ALU = mybir.AluOpType

NCORES = 8
D = 1024
DFF = 2816
NHC = DFF // 128
SEQ = 2048
NS = 16
TS = 8
PLE = 256
GCOLS = [1024 + NS * TS, 1024]
GOFF = [0, GCOLS[0]]
TCOLS = sum(GCOLS)
MAXC = max(GCOLS)
SLOT = 3072
NSLOT = 6
NORM_EPS = 1e-6
LN_EPS = 1e-5
HPARTS = [list(range(0, 11)), list(range(11, 22))]

PCOL = {}
_pc = 0


def _padd(name, n):
    global _pc
    PCOL[name] = _pc
    _pc += n


for _l in range(2):
    for _f in range(2):
        _padd(f"norm_ffn{_l}{_f}", 8)
for _l in range(2):
    _padd(f"norm_mix{_l}", 8)
    _padd(f"norm_ple{_l}", 8)
    _padd(f"norm_pp{_l}", 8)
_padd("norm_final", 8)
_padd("pool_scale", 4)
_padd("dw_bias", 4)
_padd("ln_gain", 4)
_padd("ln_bias", 4)
_padd("dw_w", 4 * 31)
_padd("sc_w", 8 * 3)
_padd("ident", 128)
NPAR = _pc


def weight_tile_list():
    tiles = []
    for l in range(2):
        def ffn(f):
            for hp, part in enumerate(HPARTS):
                for m in part:
                    tiles.append((("gu", l, f, m), 8 * 256))
                for i in range(0, len(part), 3):
                    ch = part[i:i + 3]
                    tiles.append((("dn", l, f, tuple(ch)), len(ch) * 1024))
        ffn(0)
        if l == 0:
            for j in range(4):
                tiles.append((("ine_vg", j), 8 * 256))
            for j in range(2):
                tiles.append((("ine_xa", j), 8 * 256))
            tiles.append((("pp",), 4 * 128))
            for ch in ((0, 1, 2), (3, 4, 5), (6, 7)):
                tiles.append((("oute", ch), 8 * 128 * len(ch)))
        else:
            for c in range(8):
                tiles.append((("ino", c), 8 * 384))
            for ch in ((0, 1, 2), (3, 4, 5), (6, 7)):
                tiles.append((("outo", ch), 8 * 128 * len(ch)))
        tiles.append((("plep", l), 2 * 1024))
        ffn(1)
        for ch in ((0, 1, 2), (3, 4, 5), (6, 7)):
            tiles.append((("pleg", l, ch), 8 * 128 * len(ch)))
    return tiles


def _kc(w):
    k, n = w.shape
    return w.reshape(k // 128, 128, n).transpose(1, 0, 2)


def pack_weights(inp):
    tiles = weight_tile_list()
    tot = sum(n for _, n in tiles)
    out = np.empty((128, tot), np.float32)
    off = 0
    for key, n in tiles:
        kind = key[0]
        if kind == "gu":
            _, l, f, m = key
            w = inp["w_ffn_gate_up"][l, f]
            t = np.concatenate([_kc(w[:, m * 128:(m + 1) * 128]),
                                _kc(w[:, DFF + m * 128:DFF + (m + 1) * 128])], axis=2)
        elif kind == "dn":
            _, l, f, ch = key
            w = inp["w_ffn_down"][l, f]
            t = np.stack([w[c * 128:(c + 1) * 128, :] for c in ch], axis=1)
        elif kind == "ine_vg":
            j = key[1]
            w = inp["w_in_even"][0]
            t = np.concatenate([_kc(w[:, 512 + j * 128:512 + (j + 1) * 128]),
                                _kc(w[:, 1024 + j * 128:1024 + (j + 1) * 128])], axis=2)
        elif kind == "ine_xa":
            j = key[1]
            w = inp["w_in_even"][0]
            t = _kc(w[:, j * 256:(j + 1) * 256])
        elif kind == "pp":
            t = inp["pool_proj"][0].transpose(1, 0, 2)
        elif kind == "oute":
            ch = key[1]
            t = _kc(inp["w_out_even"][0][:, ch[0] * 128:(ch[-1] + 1) * 128])
        elif kind == "ino":
            c = key[1]
            w = inp["w_in_odd"][0]
            t = np.concatenate([_kc(w[:, c * 128:(c + 1) * 128]),
                                _kc(w[:, 1024 + c * 128:1024 + (c + 1) * 128]),
                                _kc(w[:, 2048 + c * 128:2048 + (c + 1) * 128])], axis=2)
        elif kind == "outo":
            ch = key[1]
            t = _kc(inp["w_out_odd"][0][:, ch[0] * 128:(ch[-1] + 1) * 128])
        elif kind == "plep":
            t = _kc(inp["w_ple_proj"][key[1]])
        elif kind == "pleg":
            _, l, ch = key
            t = _kc(inp["w_ple_gate"][l][:, ch[0] * 128:(ch[-1] + 1) * 128])
        else:
            raise KeyError(key)
        t = np.ascontiguousarray(t).reshape(128, -1)
        assert t.shape[1] == n, (key, t.shape, n)
        out[:, off:off + n] = t
        off += n
    return out


def pack_params(inp):
    p = np.zeros((128, NPAR), np.float32)

    def put(name, vec):
        v = np.asarray(vec, np.float32).reshape(-1, 128).T
        p[:, PCOL[name]:PCOL[name] + v.shape[1]] = v

    for l in range(2):
        for f in range(2):
            put(f"norm_ffn{l}{f}", inp["norm_ffn"][l, f])
        put(f"norm_mix{l}", inp["norm_mix"][l])
        put(f"norm_ple{l}", inp["norm_ple"][l])
        put(f"norm_pp{l}", inp["norm_ple_proj"][l])
    put("norm_final", inp["norm_final"])
    put("pool_scale", inp["pool_scale"][0])
    put("dw_bias", inp["dw_bias"][0])
    put("ln_gain", inp["dw_ln_gain"][0])
    put("ln_bias", inp["dw_ln_bias"][0])
    dw = inp["dw_weight"][0]
    p[:, PCOL["dw_w"]:PCOL["dw_w"] + 124] = dw.T.reshape(4, 128, 31).transpose(1, 0, 2).reshape(128, 124)
    sc = inp["sc_weight"][0]
    p[:, PCOL["sc_w"]:PCOL["sc_w"] + 24] = sc.T.reshape(8, 128, 3).transpose(1, 0, 2).reshape(128, 24)
    p[:, PCOL["ident"]:PCOL["ident"] + 128] = np.eye(128, dtype=np.float32)
    return p


def _fm(a):
    t, f = a.shape
    return a.T.reshape(f // 128, 128, t).transpose(1, 0, 2)


def pack_core(inp, i):
    xp = inp["x_prompt"][i]
    xs = inp["x_sample"][i * NS:(i + 1) * NS]
    xs_tm = xs.transpose(1, 0, 2).reshape(NS * TS, D)
    xT = np.concatenate([_fm(xp[:1024]), _fm(xs_tm), _fm(xp[1024:])], axis=2)
    pts = []
    for l in range(2):
        pp = inp["p_prompt"][l, i]
        ps = inp["p_sample"][l, i * NS:(i + 1) * NS].transpose(1, 0, 2).reshape(NS * TS, PLE)
        pts.append(np.concatenate([_fm(pp[:1024]), _fm(ps), _fm(pp[1024:])], axis=2))
    pT = np.stack(pts, axis=1)

    def st(a):
        s, r, c = a.shape
        return np.ascontiguousarray(a.transpose(2, 1, 0).reshape(c // 128, 128, r * s).transpose(1, 0, 2))

    return {
        "xT": np.ascontiguousarray(xT, dtype=np.float32),
        "pT": np.ascontiguousarray(pT.reshape(128, 4, TCOLS), dtype=np.float32),
        "st_pool": st(inp["state_pool"][0, i * NS:(i + 1) * NS]),
        "st_dw": st(inp["state_dwconv"][0, i * NS:(i + 1) * NS]),
        "st_sc": st(inp["state_shortconv"][0, i * NS:(i + 1) * NS]),
    }


ENGS = ("pe", "act", "dve", "pool", "sp")
EPOCH = 30000


class Sched:
    def __init__(self):
        self.q = {e: [] for e in ENGS}
        self.cnt = {e: 0 for e in ENGS}
        self.segs = {}
        self.waited = {e: {} for e in ENGS}
        self.dma_cnt = {}
        self.arena = {}
        self.sem_names = set()
        self.label = ""
        self.pe_labels = []
        self.clock = {}
        self.tclock = {}

    def reg(self, handle, arena, base):
        self.arena[handle.name] = (arena, base, mybir.dt.size(handle.dtype))

    def region(self, ap):
        name = ap.tensor.name
        if name not in self.arena:
            return None
        arena, base, es = self.arena[name]
        pat = list(ap.ap)
        pstride = pat[0][0]
        off = ap.offset % pstride if pstride > 0 else ap.offset
        ext = 1
        for s, c in pat[1:]:
            ext += (c - 1) * s
        return arena, base + off * es, base + (off + ext) * es

    def _access(self, arena, lo, hi, ticket, is_write, deps):
        segs = self.segs.setdefault(arena, [])
        new = []
        covered = []
        for sg in segs:
            slo, shi, w, rs = sg
            if shi <= lo or slo >= hi:
                new.append(sg)
                continue
            if w is not None:
                deps.add(w)
            if is_write:
                deps.update(rs)
            if slo < lo:
                new.append([slo, lo, w, list(rs)])
            if shi > hi:
                new.append([hi, shi, w, list(rs)])
            olo, ohi = max(slo, lo), min(shi, hi)
            if not is_write:
                new.append([olo, ohi, w, rs + [ticket]])
                covered.append((olo, ohi))
        if is_write:
            new.append([lo, hi, ticket, []])
        else:
            covered.sort()
            cur = lo
            for olo, ohi in covered:
                if olo > cur:
                    new.append([cur, olo, None, [ticket]])
                cur = max(cur, ohi)
            if cur < hi:
                new.append([cur, hi, None, [ticket]])
        self.segs[arena] = new

    def op(self, eng, fn, reads=(), writes=(), dma_sem=None, extra_deps=(), dma_total=None):
        if dma_sem is not None:
            n = self.dma_cnt.get(dma_sem, 0) + 1
            self.dma_cnt[dma_sem] = n
            ticket = (dma_sem, 0, 16 * (dma_total if dma_total is not None else n))
            self.sem_names.add(dma_sem)
        else:
            self.cnt[eng] += 1
            ep, v = divmod(self.cnt[eng] - 1, EPOCH)
            ticket = (f"{eng}_{ep}", ep, v + 1)
            self.sem_names.add(ticket[0])
        deps = set(extra_deps)
        for ap in reads:
            r = self.region(ap)
            if r is not None:
                self._access(r[0], r[1], r[2], ticket, False, deps)
        for ap in writes:
            r = self.region(ap)
            if r is not None:
                self._access(r[0], r[1], r[2], ticket, True, deps)
        deps.discard(ticket)
        best = {}
        for (s, ep, v) in deps:
            key = s.rsplit("_", 1)[0] if s.split("_")[0] in ENGS else s
            if key not in best or (ep, v) > best[key][1:]:
                best[key] = (s, ep, v)
        waits = []
        clk = self.clock.setdefault(eng, {})
        for key, (s, ep, v) in best.items():
            if eng == "pe" and key == "pe":
                continue
            if key != eng and clk.get(key, (-1, -1)) >= (ep, v):
                continue
            prev = self.waited[eng].get(key)
            if prev is not None and prev >= (ep, v):
                continue
            self.waited[eng][key] = (ep, v)
            waits.append((s, v))
            for k2, val in self.tclock.get((s, ep, v), {}).items():
                if k2 != eng and clk.get(k2, (-1, -1)) < val:
                    clk[k2] = val
            if key != eng and clk.get(key, (-1, -1)) < (ep, v):
                clk[key] = (ep, v)
        snap = dict(clk)
        own_key = ticket[0].rsplit("_", 1)[0] if ticket[0].split("_")[0] in ENGS else ticket[0]
        snap[own_key] = (ticket[1], ticket[2])
        self.tclock[ticket] = snap
        self.q[eng].append((waits, fn, ticket))
        return ticket


def build_program():
    nc = bass.Bass("TRN2", target_bir_lowering=False)
    S = Sched()
    wtiles = weight_tile_list()
    WTOT = sum(n for _, n in wtiles)

    d_xT = nc.dram_tensor("xT", [128, 8, TCOLS], F32, kind="ExternalInput").ap()
    d_pT = nc.dram_tensor("pT", [128, 4, TCOLS], F32, kind="ExternalInput").ap()
    d_stp = nc.dram_tensor("st_pool", [128, 4, 15 * NS], F32, kind="ExternalInput").ap()
    d_stdw = nc.dram_tensor("st_dw", [128, 4, 30 * NS], F32, kind="ExternalInput").ap()
    d_stsc = nc.dram_tensor("st_sc", [128, 8, 2 * NS], F32, kind="ExternalInput").ap()
    d_par = nc.dram_tensor("params", [128, NPAR], F32, kind="ExternalInput").ap()
    d_gpr = nc.dram_tensor("gpprow", [128, 2, D], F32, kind="ExternalInput").ap()
    d_w = nc.dram_tensor("wts", [128, WTOT], F32, kind="ExternalInput").ap()
    d_yT = nc.dram_tensor("yT", [128, 8, TCOLS], F32, kind="ExternalOutput").ap()
    d_o_pp = nc.dram_tensor("o_pool_p", [128, 4, 15], F32, kind="ExternalOutput").ap()
    d_o_ps = nc.dram_tensor("o_pool_s", [128, 4, 15 * NS], F32, kind="ExternalOutput").ap()
    d_o_dp = nc.dram_tensor("o_dw_p", [128, 4, 30], F32, kind="ExternalOutput").ap()
    d_o_ds = nc.dram_tensor("o_dw_s", [128, 4, 30 * NS], F32, kind="ExternalOutput").ap()
    d_o_sp = nc.dram_tensor("o_sc_p", [128, 8, 2], F32, kind="ExternalOutput").ap()
    d_o_ss = nc.dram_tensor("o_sc_s", [128, 8, 2 * NS], F32, kind="ExternalOutput").ap()

    cur = [(nc.sbuf_base + 63) // 64 * 64]
    top = nc.sbuf_top

    def alloc(name, shape, dt, arena=None, at=None):
        nbytes = int(np.prod(shape[1:])) * mybir.dt.size(dt)
        if at is None:
            off = cur[0]
            cur[0] += (nbytes + 63) // 64 * 64
            assert cur[0] <= top, (name, cur[0], top)
        else:
            off = at
        h = nc.alloc_sbuf_tensor_at(name, list(shape), dt, offset=off)
        S.reg(h, arena or "sb", off)
        return h

    X = alloc("x", [128, 8, MAXC], F32)
    H = alloc("h", [128, 8, MAXC], BF16)
    HID_BYTES = 11 * MAXC * 2
    hid_off = cur[0]
    HID = alloc("hid", [128, 11, MAXC], BF16)
    UXP = alloc("uxp", [128, 4, 30 + 1024], BF16, at=hid_off)
    o1 = (hid_off + 4 * (30 + 1024) * 2 + 63) // 64 * 64
    UXS = alloc("uxs", [128, 4, 38 * NS], BF16, at=o1)
    o2 = (o1 + 4 * 38 * NS * 2 + 63) // 64 * 64
    V = alloc("v", [128, 4, 512], F32, at=o2)
    assert o2 + 4 * 512 * 4 <= hid_off + HID_BYTES
    YCAT = alloc("ycat", [128, 8, MAXC], BF16, at=hid_off)
    RING = alloc("ring", [128, NSLOT, SLOT], BF16)
    DIAG = alloc("diag", [128, 4 * 31, 128], BF16)
    PT = alloc("pt", [128, 4, MAXC], BF16)
    PAR = alloc("par", [128, NPAR], F32)
    GPPH = alloc("gpph", [128, 16], F32)
    ONES = alloc("ones", [128, 128], BF16)
    IDB = alloc("idb", [128, 128], BF16)
    RC = alloc("rc", [128, 16], F32)
    vb_off = cur[0]
    VB = alloc("vb", [128, 4, 512], BF16)
    VSQ = alloc("vsq", [128, 4, 512], BF16)
    ESQ2 = alloc("esq2", [128, 8, 512], BF16, at=vb_off)
    STP = alloc("stp", [128, 4, 15 * NS], F32)
    STSC = alloc("stsc", [128, 8, 2 * NS], F32)
    PTAIL = alloc("ptail", [128, 4, 16], F32)
    UTAIL = alloc("utail", [128, 4, 32], BF16)
    ZTAIL = alloc("ztail", [128, 8, 2], F32)
    O_PS = alloc("o_ps", [128, 4, TS * NS], F32)
    O_DS = alloc("o_ds", [128, 4, TS * NS], F32)
    O_DP = alloc("o_dp", [128, 4, 32], F32)
    O_SS = alloc("o_ss", [128, 8, 2 * NS], F32)
    NSCR = 10
    SCW = 528
    SCR = [alloc(f"scr{i}", [128, SCW], F32) for i in range(NSCR)]
    RSE = alloc("rse", [128, MAXC], F32)
    LRS = alloc("lrs", [128, 512], F32)
    LRS2 = alloc("lrs2", [128, 512], F32)
    NRS = [alloc("nrs0", [128, 512], F32)] * 2
    nrs_i = [0]
    scr_i = [0]

    def scratch():
        t = SCR[scr_i[0] % NSCR]
        scr_i[0] += 1
        return t

    PS = []
    for i in range(8):
        p = nc.alloc_psum_tensor(f"ps{i}", [128, 512], F32)
        S.arena[p.name] = ("ps", i * 2048, 4)
        PS.append(p)
    ps_i = [0]

    def bank():
        p = PS[ps_i[0] % 8]
        ps_i[0] += 1
        return p

    def pcol(name, c=0, n=1):
        return PAR[:, PCOL[name] + c:PCOL[name] + c + n]

    def act(out, in_, func, bias=None, scale=None, extra_reads=()):
        kw = {}
        rd = [in_] + list(extra_reads)
        if bias is not None:
            kw["bias"] = bias
            if not isinstance(bias, float):
                rd.append(bias)
        if scale is not None:
            kw["scale"] = scale
            if not isinstance(scale, float):
                rd.append(scale)
        S.op("act", lambda e: e.activation(out=out, in_=in_, func=func, **kw), reads=rd, writes=[out])

    def dve_tt(out, in0, in1, op):
        S.op("dve", lambda e: e.tensor_tensor(out=out, in0=in0, in1=in1, op=op), reads=[in0, in1], writes=[out])

    def dve_stt(out, in0, scalar, in1, op0, op1):
        rd = [in0, in1] + ([] if isinstance(scalar, float) else [scalar])
        S.op("dve", lambda e: e.scalar_tensor_tensor(out=out, in0=in0, scalar=scalar, in1=in1, op0=op0, op1=op1),
             reads=rd, writes=[out])

    def dve_ts(out, in0, s1, s2, op0, op1=None):
        rd = [in0] + [s for s in (s1, s2) if s is not None and not isinstance(s, float)]
        if op1 is None:
            S.op("dve", lambda e: e.tensor_scalar(out=out, in0=in0, scalar1=s1, scalar2=None, op0=op0),
                 reads=rd, writes=[out])
        else:
            S.op("dve", lambda e: e.tensor_scalar(out=out, in0=in0, scalar1=s1, scalar2=s2, op0=op0, op1=op1),
                 reads=rd, writes=[out])

    def pool_tt(out, in0, in1, op):
        S.op("pool", lambda e: e.tensor_tensor(out=out, in0=in0, in1=in1, op=op), reads=[in0, in1], writes=[out])

    def dve_copy(out, in_):
        S.op("dve", lambda e: e.tensor_copy(out=out, in_=in_), reads=[in_], writes=[out])

    def dve_recip(out, in_):
        S.op("dve", lambda e: e.reciprocal(out=out, in_=in_), reads=[in_], writes=[out])

    def dve_memset(out, val):
        S.op("dve", lambda e: e.memset(out, val), writes=[out])

    PE_FW = [None]

    def mm_group(ps_ap, pairs, extra_reads=()):
        n = len(pairs)

        def fn(e):
            ins = None
            for i, (l, r) in enumerate(pairs):
                ins = e.matmul(ps_ap, l, r, start=(i == 0), stop=(i == n - 1))
                if i == 0 and PE_FW[0] is not None:
                    ins._wait_ge(*PE_FW[0])
                    PE_FW[0] = None
            return ins
        rd = [a for pr in pairs for a in pr] + list(extra_reads)
        S.pe_labels.extend([S.label] * n)
        S.op("pe", fn, reads=rd, writes=[ps_ap])

    def dma(eng, out, in_, sem, total=None, extra=()):
        S.op(eng, lambda e: e.dma_start(out=out, in_=in_), reads=[in_], writes=[out], dma_sem=sem, dma_total=total,
             extra_deps=extra)

    def dma_group(eng, pairs, sem):
        tot = S.dma_cnt.get(sem, 0) + len(pairs)
        for (out, in_) in pairs:
            dma(eng, out, in_, sem, total=tot)

    wt_off = {}
    o = 0
    for key, n in wtiles:
        wt_off[key] = (o, n)
        o += n
    class WStream:
        def __init__(self):
            self.order = [k for k, _ in wtiles]
            self.reset()

        def reset(self):
            self.pos = 0
            self.loaded = {}
            self.free = list(range(NSLOT))

        def _load_next(self):
            k = self.order[self.pos]
            s = self.free.pop(0)
            off, n = wt_off[k]
            extra = (("xsem0", 0, 128),) if not first_w_issued else ()
            first_w_issued.append(1)
            dma("pool", RING[:, s, 0:n], d_w[:, off:off + n], f"wsem{s}", extra=extra)
            self.loaded[k] = s
            self.pos += 1

        def get(self, key, ahead):
            idx = self.order.index(key)
            while self.pos <= idx:
                assert self.free, ("no free weight slot for", key)
                self._load_next()
            while self.pos < len(self.order) and self.pos <= idx + ahead and self.free:
                self._load_next()
            return self.loaded[key]

        def done(self, key):
            self.free.append(self.loaded.pop(key))

    first_w_issued = []
    W = WStream()

    G0_BLOCKS = [(0, 512), (512, 512), (1024, 128)]
    for bi, (c0_, n_) in enumerate(G0_BLOCKS):
        dma_group("sp", [(X[:, c, c0_:c0_ + n_], d_xT[:, c, GOFF[0] + c0_:GOFF[0] + c0_ + n_]) for c in range(8)],
                  f"xsem{bi}")
        if bi == 0:
            dma("sp", PAR[:, :], d_par[:, :], "psem")
    dve_memset(ONES[:, :], 1.0)
    dve_copy(IDB[:, :], PAR[:, PCOL["ident"]:PCOL["ident"] + 128])
    for l in range(2):
        dve_ts(GPPH[:, l * 8:(l + 1) * 8], PAR[:, PCOL[f"norm_pp{l}"]:PCOL[f"norm_pp{l}"] + 8], 0.5, None, ALU.mult)
    late = []
    for t in range(15):
        late.append(lambda t=t: dve_memset(RC[:, t:t + 1], 1.0 / (t + 1)))
    for j in range(4):
        for k in range(31):
            late.append(lambda j=j, k=k: dve_ts(DIAG[:, j * 31 + k, :], IDB[:, :], pcol("dw_w", j * 31 + k), None, ALU.mult))
    late.append(lambda: dma_group("sp", [(STP[:, :, :], d_stp[:, :, :]), (STSC[:, :, :], d_stsc[:, :, :])], "stsem"))
    late.append(lambda: dma("sp", d_o_ps[:, :, 0:7 * NS], d_stp[:, :, 8 * NS:15 * NS], "osem_a"))
    late.append(lambda: dma("sp", d_o_ds[:, :, 0:22 * NS], d_stdw[:, :, 8 * NS:30 * NS], "osem_a"))

    def late_tick(k=4):
        for _ in range(k):
            if late:
                late.pop(0)()

    pending = []
    cur_goff = [0]
    cur_g = [0]

    G1_BLOCKS = [(0, 512), (512, 384), (896, 128)]

    def flush():
        lab = S.label
        while pending:
            pending.pop(0)()
        S.label = lab

    def tick(k=2):
        lab = S.label
        for _ in range(k):
            if not pending:
                break
            pending.pop(0)()
        S.label = lab

    def norm_squares(blk):
        c0, n = blk
        for c in range(8):
            act(H[:, c, c0:c0 + n], X[:, c, c0:c0 + n], AF.Square)

    def norm_steps(blk, gname, final, g_at):
        c0, n = blk
        rs_box = []
        steps = []

        def s_stats():
            S.label = "norm_" + gname
            pb = bank()
            mm_group(pb[:, 0:n], [(ONES[:, :], H[:, c, c0:c0 + n]) for c in range(8)])
            sd = scratch()
            act(sd[:, 0:n], pb[:, 0:n], AF.Sqrt, bias=NORM_EPS, scale=1.0 / D)
            rs = LRS2 if final else NRS[nrs_i[0] % 2]
            nrs_i[0] += 1
            dve_recip(rs[:, 0:n], sd[:, 0:n])
            rs_box.append(rs)
        steps.append(s_stats)

        def mk(c):
            def s_scale():
                rs = rs_box[0]
                if not final:
                    dve_stt(H[:, c, c0:c0 + n], X[:, c, c0:c0 + n], pcol(gname, c), rs[:, 0:n], ALU.mult, ALU.mult)
                else:
                    dve_stt(X[:, c, c0:c0 + n], X[:, c, c0:c0 + n], pcol(gname, c), rs[:, 0:n], ALU.mult, ALU.mult)
                    sem = f"yo{c}_{c0}" if (g_at == 0 and c0 + n <= GCOLS[1]) else "ysem"
                    dma("sp", d_yT[:, c, GOFF[g_at] + c0:GOFF[g_at] + c0 + n], X[:, c, c0:c0 + n], sem)
            return s_scale
        for c in range(8):
            steps.append(mk(c))

        if final and g_at == 0 and c0 + n <= GCOLS[1]:
            def s_next():
                dma_group("sp", [(X[:, c, c0:c0 + n], d_xT[:, c, GOFF[1] + c0:GOFF[1] + c0 + n]) for c in range(8)],
                          f"xg1_{c0}")
                pending.extend([(lambda: None)] * 10)
                for (b0, bn) in G1_BLOCKS:
                    if c0 <= b0 and b0 + bn <= c0 + n:
                        pending.append(lambda b0=b0, bn=bn: x_block_final((b0, bn), "norm_ffn00", g_next=1))
            steps.append(s_next)
        return steps

    def x_block_final(blk, gname, g_next=None):
        norm_squares(blk)
        final = (gname == "norm_final")
        g_at = cur_g[0] if g_next is None else g_next
        pending.extend(norm_steps(blk, gname, final, g_at))

    def tb_order(nt, nb):
        lead = min(4, nt)
        out = [(0, 0), "F"]
        out += [(0, b) for b in range(1, nb - 1)]
        for t in range(1, lead):
            out += [(t, b) for b in range(0, nb - 1)]
        out += [(t, nb - 1) for t in range(lead)]
        for t in range(lead, nt):
            out += [(t, b) for b in range(nb)]
        return out

    def last_use(order):
        lu = {}
        for i, it in enumerate(order):
            if it != "F":
                lu[it[0]] = i
        return lu

    def emb_stats_mm(l, wp, blk):
        c0, n = blk
        for mo in range(8):
            pb = bank()
            mm_group(pb[:, 0:n], [(wp[:, k, mo * 128:(mo + 1) * 128], PT[:, l * 2 + k, c0:c0 + n]) for k in range(2)])
            act(ESQ2[:, mo, 0:n], pb[:, 0:n], AF.Square)

    fin_l = [0]

    def emb_stats_fin(blk):
        c0, n = blk
        pb = bank()
        mm_group(pb[:, 0:n], [(ONES[:, :], ESQ2[:, mo, 0:n]) for mo in range(8)])
        sd = scratch()
        act(sd[:, 0:n], pb[:, 0:n], AF.Sqrt, bias=NORM_EPS, scale=1.0 / D)
        dve_recip(RSE[:, c0:c0 + n], sd[:, 0:n])
        for k in range(2):
            dve_tt(PT[:, fin_l[0] * 2 + k, c0:c0 + n], PT[:, fin_l[0] * 2 + k, c0:c0 + n], RSE[:, c0:c0 + n], ALU.mult)

    def ffn(l, f, blocks, next_g):
        nb = len(blocks)
        wp = None
        if f == 1:
            sp_ = W.get(("plep", l), 2)
            wp = RING[:, sp_, 0:2048].rearrange("p (k n) -> p k n", k=2)
        for pi, part in enumerate(HPARTS):
            S.label = f"gu{l}{f}"
            order = tb_order(len(part), nb) if pi == 0 else [(t, b) for t in range(len(part)) for b in range(nb)]
            lu = last_use(order)
            wts_ = {}
            for oi, item in enumerate(order):
                if item == "F":
                    flush()
                    continue
                j, bi = item
                if j not in wts_:
                    s = W.get(("gu", l, f, part[j]), 2)
                    wts_[j] = RING[:, s, 0:2048].rearrange("p (k n) -> p k n", k=8)
                wt = wts_[j]
                c0, n = blocks[bi]
                pg, pu = bank(), bank()
                mm_group(pg[:, 0:n], [(wt[:, k, 0:128], H[:, k, c0:c0 + n]) for k in range(8)])
                mm_group(pu[:, 0:n], [(wt[:, k, 128:256], H[:, k, c0:c0 + n]) for k in range(8)])
                sg = scratch()
                act(sg[:, 0:n], pg[:, 0:n], AF.Silu)
                dve_tt(HID[:, j, c0:c0 + n], pu[:, 0:n], sg[:, 0:n], ALU.mult)
                if oi >= 3 or pi > 0 or f > 0 or l > 0:
                    late_tick()
                if lu[j] == oi:
                    W.done(("gu", l, f, part[j]))
            S.label = f"dn{l}{f}"
            wdn = {}
            dn_keys = []
            for i in range(0, len(part), 3):
                ch = tuple(part[i:i + 3])
                s = W.get(("dn", l, f, ch), 1)
                dn_keys.append(("dn", l, f, ch))
                for ii, cc in enumerate(ch):
                    wdn[cc] = RING[:, s, ii * 1024:(ii + 1) * 1024]

            def dn_item(mo, blk):
                c0, n = blk
                pb = bank()
                mm_group(pb[:, 0:n], [(wdn[m][:, mo * 128:(mo + 1) * 128], HID[:, j, c0:c0 + n])
                                      for j, m in enumerate(part)])
                dve_stt(X[:, mo, c0:c0 + n], pb[:, 0:n], 0.5, X[:, mo, c0:c0 + n], ALU.mult, ALU.add)

            if pi < len(HPARTS) - 1:
                for mo in range(8):
                    for blk in blocks:
                        dn_item(mo, blk)
                    if f == 1:
                        S.label = f"embst{l}"
                        if 1 <= mo and mo - 1 < nb:
                            fin_l[0] = l
                            emb_stats_fin(blocks[mo - 1])
                        if mo < nb:
                            emb_stats_mm(l, wp, blocks[mo])
                        S.label = f"dn{l}{f}"
            else:
                for blk in blocks:
                    for mo in range(8):
                        dn_item(mo, blk)
                        tick()
                    x_block_final(blk, next_g)
            for k in dn_keys:
                W.done(k)

    def even_mixer(g, blocks, next_g):
        nb = len(blocks)
        late_tick(10 ** 6)
        last_prompt = (g == 1)
        if g == 0:
            dve_memset(UXP[:, :, 0:30], 0.0)
            dma("pool", UXS[:, :, 0:30 * NS], d_stdw[:, :, :], "stsem2")
            dve_memset(PTAIL[:, :, :], 0.0)
        else:
            dve_copy(UXP[:, :, 0:30], UTAIL[:, :, 0:30])
        S.label = "even_vg"
        wts_ = {}
        order = tb_order(4, nb)
        lu = last_use(order)
        for oi, item in enumerate(order):
            if item == "F":
                flush()
                continue
            j, bi = item
            if j not in wts_:
                s = W.get(("ine_vg", j), 2)
                wts_[j] = RING[:, s, 0:2048].rearrange("p (k n) -> p k n", k=8)
            wt = wts_[j]
            c0, n = blocks[bi]
            is_s = (g == 0 and bi == 2)
            pv, pg = bank(), bank()
            mm_group(pv[:, 0:n], [(wt[:, k, 0:128], H[:, k, c0:c0 + n]) for k in range(8)])
            mm_group(pg[:, 0:n], [(wt[:, k, 128:256], H[:, k, c0:c0 + n]) for k in range(8)])
            th, vh = scratch(), scratch()
            act(th[:, 0:n], pg[:, 0:n], AF.Tanh, scale=0.5)
            act(vh[:, 0:n], pv[:, 0:n], AF.Copy, scale=0.5)
            if is_s:
                dst = UXS[:, j, 30 * NS:38 * NS]
            else:
                dst = UXP[:, j, 30 + c0:30 + c0 + n]
            dve_stt(dst, th[:, 0:n], 1.0, vh[:, 0:n], ALU.add, ALU.mult)
            if is_s:
                dve_stt(O_DS[:, j, :], th[:, 0:n], 1.0, vh[:, 0:n], ALU.add, ALU.mult)
            if last_prompt and bi == nb - 1:
                dve_stt(O_DP[:, j, 0:30], th[:, n - 30:n], 1.0, vh[:, n - 30:n], ALU.add, ALU.mult)
            if g == 0 and bi == 1:
                dve_stt(UTAIL[:, j, 0:30], th[:, n - 30:n], 1.0, vh[:, n - 30:n], ALU.add, ALU.mult)
            if lu[j] == oi:
                W.done(("ine_vg", j))
        S.label = "even_xa"
        sp = W.get(("pp",), 0)
        wpp = RING[:, sp, 0:512].rearrange("p (g d) -> p g d", g=4)
        wxa = []
        for jj in range(2):
            s = W.get(("ine_xa", jj), 0)
            wxa.append(RING[:, s, 0:2048].rearrange("p (k n) -> p k n", k=8))

        def pool_item(c, bi, c0, n, pb):
            is_s = (g == 0 and bi == 2)
            st = NS if is_s else 1
            P = 15 * st
            w = 2 << c
            xe = scratch()
            if is_s:
                dve_copy(xe[:, 0:P], STP[:, c, :])
            else:
                dve_copy(xe[:, 0:P], PTAIL[:, c, 0:15])
            act(xe[:, P:P + n], pb[:, 0:n], AF.Copy)
            if is_s:
                dve_copy(O_PS[:, c, :], xe[:, P:P + n])
            else:
                dve_copy(PTAIL[:, c, 0:15], xe[:, n:n + 15])
            L = P + n
            src = xe
            step = 1
            while step < w:
                dst = scratch()
                lo = (2 * step - 1) * st
                dve_tt(dst[:, lo:L], src[:, lo:L], src[:, lo - step * st:L - step * st], ALU.add)
                src = dst
                step *= 2
            dve_stt(H[:, c, c0:c0 + n], src[:, P:L], 1.0 / w, xe[:, P:L], ALU.mult, ALU.subtract)
            if g == 0 and bi == 0:
                tmp = scratch()
                dve_tt(tmp[:, 0:w - 1], src[:, P:P + w - 1], RC[:, 0:w - 1], ALU.mult)
                dve_tt(H[:, c, c0:c0 + w - 1], tmp[:, 0:w - 1], xe[:, P:P + w - 1], ALU.subtract)

        for bi, (c0, n) in enumerate(blocks):
            pbs = []
            for c in range(4):
                pb = bank()
                mm_group(pb[:, 0:n], [(wxa[c // 2][:, k, (c % 2) * 128:(c % 2 + 1) * 128], H[:, k, c0:c0 + n])
                                      for k in range(8)])
                pbs.append(pb)
            for c in range(4):
                pool_item(c, bi, c0, n, pbs[c])
        W.done(("ine_xa", 0))
        W.done(("ine_xa", 1))
        def pool_proj_blk(blk_):
            S.label = "even_pp"
            for (c0, n) in [blk_]:
                for c in range(4):
                    pq = bank()
                    mm_group(pq[:, 0:n], [(wpp[:, c, :], H[:, c, c0:c0 + n])])
                    act(H[:, c, c0:c0 + n], pq[:, 0:n], AF.Copy, scale=pcol("pool_scale", c))
            S.label = "even_conv"

        for ci_, ch_ in enumerate(((0, 1, 2), (3, 4, 5), (6, 7))):
            W.get(("oute", ch_), 0)
        S.label = "even_conv"
        for bi, (c0, n) in enumerate(blocks):
            is_s = (g == 0 and bi == 2)
            for j in range(4):
                pb = bank()
                if is_s:
                    pairs = [(DIAG[:, j * 31 + k, :], UXS[:, j, k * NS:k * NS + n]) for k in range(31)]
                else:
                    pairs = [(DIAG[:, j * 31 + k, :], UXP[:, j, c0 + k:c0 + k + n]) for k in range(31)]
                mm_group(pb[:, 0:n], pairs)
                act(V[:, j, 0:n], pb[:, 0:n], AF.Identity, bias=pcol("dw_bias", j))
                act(VB[:, j, 0:n], pb[:, 0:n], AF.Identity, bias=pcol("dw_bias", j))
                act(VSQ[:, j, 0:n], pb[:, 0:n], AF.Square, bias=pcol("dw_bias", j))
            if bi >= 1:
                pool_proj_blk(blocks[bi - 1])
            if bi == nb - 1:
                pool_proj_blk(blocks[bi])
            p1, p2 = bank(), bank()
            mm_group(p1[:, 0:n], [(ONES[:, :], VB[:, j, 0:n]) for j in range(4)])
            mm_group(p2[:, 0:n], [(ONES[:, :], VSQ[:, j, 0:n]) for j in range(4)])
            mu, msq, var, sd, rs = scratch(), scratch(), scratch(), scratch(), scratch()
            dve_ts(mu[:, 0:n], p1[:, 0:n], 1.0 / 512, None, ALU.mult)
            dve_tt(msq[:, 0:n], mu[:, 0:n], mu[:, 0:n], ALU.mult)
            dve_stt(var[:, 0:n], p2[:, 0:n], 1.0 / 512, msq[:, 0:n], ALU.mult, ALU.subtract)
            act(sd[:, 0:n], var[:, 0:n], AF.Sqrt, bias=LN_EPS)
            dve_recip(rs[:, 0:n], sd[:, 0:n])
            for j in range(4):
                dve_tt(V[:, j, 0:n], V[:, j, 0:n], mu[:, 0:n], ALU.subtract)
                dve_tt(V[:, j, 0:n], V[:, j, 0:n], rs[:, 0:n], ALU.mult)
                act(H[:, 4 + j, c0:c0 + n], V[:, j, 0:n], AF.Silu, bias=pcol("ln_bias", j), scale=pcol("ln_gain", j))
        W.done(("pp",))
        S.label = "even_out"
        wo = {}
        for ci, ch in enumerate(((0, 1, 2), (3, 4, 5), (6, 7))):
            s = W.get(("oute", ch), (2, 1, 1)[ci])
            wt = RING[:, s, 0:8 * 128 * len(ch)].rearrange("p (k n) -> p k n", k=8)
            for ii, mo in enumerate(ch):
                wo[mo] = (wt, ii)
        for blk in blocks:
            c0, n = blk
            for mo in range(8):
                wt, ii = wo[mo]
                pb = bank()
                mm_group(pb[:, 0:n], [(wt[:, k, ii * 128:(ii + 1) * 128], H[:, k, c0:c0 + n]) for k in range(8)])
                dve_tt(X[:, mo, c0:c0 + n], pb[:, 0:n], X[:, mo, c0:c0 + n], ALU.add)
                tick()
            x_block_final(blk, next_g)
        for ch in ((0, 1, 2), (3, 4, 5), (6, 7)):
            W.done(("oute", ch))
        if last_prompt:
            dma("sp", d_o_dp[:, :, :], O_DP[:, :, 0:30], "osem_b")
            dma("sp", d_o_pp[:, :, :], PTAIL[:, :, 0:15], "osem_b")
        if g == 0:
            dma("sp", d_o_ds[:, :, 22 * NS:30 * NS], O_DS[:, :, :], "osem_b")
            dma("sp", d_o_ps[:, :, 7 * NS:15 * NS], O_PS[:, :, :], "osem_b")

    def odd_mixer(g, blocks, next_g):
        nb = len(blocks)
        last_prompt = (g == 1)
        if g == 0:
            dve_memset(ZTAIL[:, :, :], 0.0)
        S.label = "odd_in"
        wts_ = {}
        order = tb_order(8, nb)
        lu = last_use(order)
        for oi, item in enumerate(order):
            if item == "F":
                flush()
                continue
            c, bi = item
            if c not in wts_:
                s = W.get(("ino", c), 2)
                wts_[c] = RING[:, s, 0:3072].rearrange("p (k n) -> p k n", k=8)
            wt = wts_[c]
            c0, n = blocks[bi]
            is_s = (g == 0 and bi == 2)
            st = NS if is_s else 1
            P = 2 * st
            pgb, pgc, pxv = bank(), bank(), bank()
            mm_group(pgb[:, 0:n], [(wt[:, k, 0:128], H[:, k, c0:c0 + n]) for k in range(8)])
            mm_group(pgc[:, 0:n], [(wt[:, k, 128:256], H[:, k, c0:c0 + n]) for k in range(8)])
            mm_group(pxv[:, 0:n], [(wt[:, k, 256:384], H[:, k, c0:c0 + n]) for k in range(8)])
            gcs, gbs, ze, a1, a2 = scratch(), scratch(), scratch(), scratch(), scratch()
            act(gcs[:, 0:n], pgc[:, 0:n], AF.Copy)
            act(gbs[:, 0:n], pgb[:, 0:n], AF.Copy)
            if is_s:
                dve_copy(ze[:, 0:P], STSC[:, c, :])
            else:
                dve_copy(ze[:, 0:P], ZTAIL[:, c, :])
            dve_tt(ze[:, P:P + n], pxv[:, 0:n], gcs[:, 0:n], ALU.mult)
            if is_s:
                dve_copy(O_SS[:, c, :], ze[:, P + 6 * NS:P + 8 * NS])
            else:
                dve_copy(ZTAIL[:, c, :], ze[:, n:n + 2])
            dve_ts(a1[:, 0:n], ze[:, 0:n], pcol("sc_w", c * 3 + 0), None, ALU.mult)
            dve_stt(a2[:, 0:n], ze[:, st:st + n], pcol("sc_w", c * 3 + 1), a1[:, 0:n], ALU.mult, ALU.add)
            dve_stt(a1[:, 0:n], ze[:, 2 * st:2 * st + n], pcol("sc_w", c * 3 + 2), a2[:, 0:n], ALU.mult, ALU.add)
            dve_tt(YCAT[:, c, c0:c0 + n], a1[:, 0:n], gbs[:, 0:n], ALU.mult)
            if lu[c] == oi:
                W.done(("ino", c))
        S.label = "odd_out"
        wo = {}
        for ci, ch in enumerate(((0, 1, 2), (3, 4, 5), (6, 7))):
            s = W.get(("outo", ch), (2, 1, 1)[ci])
            wt = RING[:, s, 0:8 * 128 * len(ch)].rearrange("p (k n) -> p k n", k=8)
            for ii, mo in enumerate(ch):
                wo[mo] = (wt, ii)
        for blk in blocks:
            c0, n = blk
            for mo in range(8):
                wt, ii = wo[mo]
                pb = bank()
                mm_group(pb[:, 0:n], [(wt[:, k, ii * 128:(ii + 1) * 128], YCAT[:, k, c0:c0 + n]) for k in range(8)])
                dve_tt(X[:, mo, c0:c0 + n], pb[:, 0:n], X[:, mo, c0:c0 + n], ALU.add)
                tick()
            x_block_final(blk, next_g)
        for ch in ((0, 1, 2), (3, 4, 5), (6, 7)):
            W.done(("outo", ch))
        if last_prompt:
            dma("sp", d_o_sp[:, :, :], ZTAIL[:, :, :], "osem_b")
        if g == 0:
            dma("sp", d_o_ss[:, :, :], O_SS[:, :, :], "osem_b")

    def ple(l, blocks, next_g):
        S.label = f"ple{l}"
        sp_ = W.get(("plep", l), 0)
        wp = RING[:, sp_, 0:2048].rearrange("p (k n) -> p k n", k=2)
        wg = {}
        gkeys = []
        for ci, ch in enumerate(((0, 1, 2), (3, 4, 5), (6, 7))):
            s = W.get(("pleg", l, ch), (2, 1, 1)[ci])
            gkeys.append(("pleg", l, ch))
            wt = RING[:, s, 0:8 * 128 * len(ch)].rearrange("p (k n) -> p k n", k=8)
            for ii, mo in enumerate(ch):
                wg[mo] = (wt, ii)
        for hh in range(2):
            gr = scratch()
            dma("sp", gr[:, 0:512], d_gpr[:, l, hh * 512:(hh + 1) * 512], f"gpr{hh}")
            dve_ts(gr[:, 0:512], gr[:, 0:512], 0.5, None, ALU.mult)
            for k in range(2):
                dve_tt(wp[:, k, hh * 512:(hh + 1) * 512], wp[:, k, hh * 512:(hh + 1) * 512], gr[:, 0:512], ALU.mult)
        for bi, blk in enumerate(blocks):
            c0, n = blk
            for mo in range(8):
                wt, ii = wg[mo]
                pe_, pz = bank(), bank()
                mm_group(pe_[:, 0:n], [(wp[:, k, mo * 128:(mo + 1) * 128], PT[:, l * 2 + k, c0:c0 + n]) for k in range(2)])
                mm_group(pz[:, 0:n], [(wt[:, k, ii * 128:(ii + 1) * 128], H[:, k, c0:c0 + n]) for k in range(8)])
                th, t1 = scratch(), scratch()
                act(th[:, 0:n], pz[:, 0:n], AF.Tanh, scale=0.5)
                dve_stt(t1[:, 0:n], th[:, 0:n], 1.0, pe_[:, 0:n], ALU.add, ALU.mult)
                pool_tt(X[:, mo, c0:c0 + n], X[:, mo, c0:c0 + n], t1[:, 0:n], ALU.add)
                if bi == 0 and mo == 0:
                    flush()
                else:
                    tick()
            x_block_final(blk, next_g)
        W.done(("plep", l))
        for k in gkeys:
            W.done(k)

    for g in range(2):
        nco = GCOLS[g]
        cur_goff[0] = GOFF[g]
        cur_g[0] = g
        blocks = [(0, 512), (512, 512), (1024, 128)] if g == 0 else G1_BLOCKS
        W.reset()
        if g == 0:
            late.insert(0, lambda: dma_group("pool", [(PT[:, c, 0:GCOLS[0]], d_pT[:, c, GOFF[0]:GOFF[0] + GCOLS[0]])
                                                      for c in range(4)], "ptsem"))
        else:
            dma_group("pool", [(PT[:, c, 0:nco], d_pT[:, c, GOFF[g]:GOFF[g] + nco]) for c in range(4)], "ptsem")
        if g == 0:
            for blk in blocks:
                x_block_final(blk, "norm_ffn00")
            keep = pending[-9:]
            del pending[-9:]
            flush()
            pending.extend(keep)
        for l in range(2):
            ffn(l, 0, blocks, f"norm_mix{l}")
            if l == 0:
                even_mixer(g, blocks, "norm_ffn01")
            else:
                odd_mixer(g, blocks, "norm_ffn11")
            ffn(l, 1, blocks, f"norm_ple{l}")
            ple(l, blocks, "norm_ffn10" if l == 0 else "norm_final")
        flush()
        flush()

    out_sems = ["osem_a", "osem_b"] + sorted(n_ for n_ in S.sem_names if n_.startswith("ysem") or n_.startswith("yo"))
    sem_objs = {}
    sem_list = sorted(S.sem_names)
    import contextlib
    with contextlib.ExitStack() as stack:
        for sname in sem_list:
            sem_objs[sname] = stack.enter_context(nc.semaphore(sname))
        block = stack.enter_context(nc.Block())

        def run_engine(e, name):
            for waits, fn, ticket in S.q[name]:
                fused = None
                if name == "pe" and waits:
                    PE_FW[0] = (sem_objs[waits[-1][0]], waits[-1][1])
                    waits = waits[:-1]
                if name in ("act", "dve") and waits:
                    fused = waits[-1]
                    waits = waits[:-1]
                for (sn, v) in waits:
                    e.wait_ge(sem_objs[sn], v)
                ins = fn(e)
                if fused is not None:
                    ins._wait_ge(sem_objs[fused[0]], fused[1])
                if ticket[0].split("_")[0] in ENGS:
                    ins.then_inc(sem_objs[ticket[0]], 1)
                else:
                    ins.then_inc(sem_objs[ticket[0]], 16)
            if name == "sp":
                for sn in out_sems:
                    e.wait_ge(sem_objs[sn], 16 * S.dma_cnt[sn])

        block.tensor(lambda e: run_engine(e, "pe"))
        block.scalar(lambda e: run_engine(e, "act"))
        block.vector(lambda e: run_engine(e, "dve"))
        block.gpsimd(lambda e: run_engine(e, "pool"))
        block.sync(lambda e: run_engine(e, "sp"))
    nc._pe_labels = S.pe_labels
    return nc


_CACHE = {}


def kernel(**inputs):
    inp = {k: np.asarray(v) for k, v in inputs.items()}
    if "nc" not in _CACHE:
        _CACHE["nc"] = build_program()
    nc = _CACHE["nc"]
    wts = pack_weights(inp)
    par = pack_params(inp)
    gpr = np.ascontiguousarray(np.broadcast_to(inp["norm_ple_proj"].astype(np.float32)[None, :, :], (128, 2, D)))
    in_maps = []
    for i in range(NCORES):
        m = pack_core(inp, i)
        m["params"] = par
        m["gpprow"] = gpr
        m["wts"] = wts
        in_maps.append(m)
    res = run_bass_kernel_spmd(nc, in_maps, core_ids=list(range(NCORES)))
    R = res.results

    def unfm(a):
        return a.transpose(2, 1, 0).reshape(a.shape[2], -1)

    y_prompt = np.empty((NCORES, SEQ, D), np.float32)
    y_sample = np.empty((NCORES * NS, TS, D), np.float32)
    pool_p = np.empty((1, NCORES, 15, 512), np.float32)
    pool_s = np.empty((1, NCORES * NS, 15, 512), np.float32)
    dw_p = np.empty((1, NCORES, 30, 512), np.float32)
    dw_s = np.empty((1, NCORES * NS, 30, 512), np.float32)
    sc_p = np.empty((1, NCORES, 2, 1024), np.float32)
    sc_s = np.empty((1, NCORES * NS, 2, 1024), np.float32)
    for i in range(NCORES):
        yT = np.asarray(R[i]["yT"])
        y_prompt[i, :1024] = unfm(yT[:, :, 0:1024])
        y_prompt[i, 1024:] = unfm(yT[:, :, GOFF[1]:GOFF[1] + 1024])
        ys = unfm(yT[:, :, 1024:1024 + NS * TS]).reshape(TS, NS, D).transpose(1, 0, 2)
        y_sample[i * NS:(i + 1) * NS] = ys

        def st_s(a, rows):
            return unfm(np.asarray(a)).reshape(rows, NS, -1).transpose(1, 0, 2)

        pool_p[0, i] = unfm(np.asarray(R[i]["o_pool_p"]))
        dw_p[0, i] = unfm(np.asarray(R[i]["o_dw_p"]))
        sc_p[0, i] = unfm(np.asarray(R[i]["o_sc_p"]))
        pool_s[0, i * NS:(i + 1) * NS] = st_s(R[i]["o_pool_s"], 15)
        dw_s[0, i * NS:(i + 1) * NS] = st_s(R[i]["o_dw_s"], 30)
        sc_s[0, i * NS:(i + 1) * NS] = st_s(R[i]["o_sc_s"], 2)
    return (y_prompt, y_sample, pool_p, pool_s, dw_p, dw_s, sc_p, sc_s)
```
